# Optimizing a Trainium2 kernel written in Bass

```python
import numpy as np
import jax, jax.numpy as jnp
from jax import lax

D_MODEL = 4096
BATCH = 4
SEQ = 2048
DEPTH = 2

HEAD_DIM = 128
H_GDN = 16
GDN_CONV = 4
GDN_CHUNK = 64
H_NSA = 16
G_NSA = 4
HPG_NSA = H_NSA // G_NSA
L_CMP = 32
S_CMP = 16
L_SLC = 64
N_SEL = 8
WINDOW = 512
SLC_QBLOCK = 64
FORCE_SCORE = 1e4
H_HGRN = 16
HGRN_CHUNK = 64
H_FOX = 16
ATT_QBLOCK = 128
D_FF = 11008
FFN_CONV = 3
D_PLE = 256
EPS = 1e-6

N_AB = (DEPTH + 1) // 2
N_CD = DEPTH // 2
AB_SPLITS = [3 * H_GDN * HEAD_DIM, H_GDN, H_GDN, H_GDN * HEAD_DIM,
             H_NSA * HEAD_DIM, 6 * G_NSA * HEAD_DIM, 3 * H_NSA]
CD_SPLITS = [H_HGRN * HEAD_DIM] * 4 + [3 * H_FOX * HEAD_DIM, H_FOX]
AB_IN = sum(AB_SPLITS)
CD_IN = sum(CD_SPLITS)
AB_OUT = (H_GDN + H_NSA) * HEAD_DIM
CD_OUT = (H_HGRN + H_FOX) * HEAD_DIM

kernel_name = 'hybrid_gdn_nsa_hgrn2_fox_block'

F32 = jnp.float32


def rmsnorm(z, w):
    zf = z.astype(F32)
    y = zf * lax.rsqrt(jnp.mean(zf * zf, -1, keepdims=True) + EPS)
    return (y * w.astype(F32)).astype(z.dtype)


def l2norm(z):
    return z * lax.rsqrt(jnp.sum(z * z, -1, keepdims=True) + EPS)


def masked_softmax(s, mask):
    s = jnp.where(mask, s, -jnp.inf)
    m = jnp.max(s, -1, keepdims=True)
    e = jnp.exp(s - jnp.where(jnp.isfinite(m), m, 0.0))
    d = jnp.sum(e, -1, keepdims=True)
    return e / jnp.where(d > 0, d, 1.0)


def causal_dwconv(z, w):
    k, t = w.shape[0], z.shape[1]
    zp = jnp.pad(z, ((0, 0), (k - 1, 0), (0, 0)))
    y = zp[:, :t] * w[0]
    for j in range(1, k):
        y = y + zp[:, j:j + t] * w[j]
    return y


def split_cols(z, sizes):
    return jnp.split(z, np.cumsum(sizes)[:-1].tolist(), axis=-1)


def split_blocks(z, axis, size):
    shp = z.shape
    z = z.reshape(shp[:axis] + (shp[axis] // size, size) + shp[axis + 1:])
    return jnp.moveaxis(z, axis, 0)


def merge_blocks(z, axis):
    z = jnp.moveaxis(z, 0, axis)
    shp = z.shape
    return z.reshape(shp[:axis] + (shp[axis] * shp[axis + 1],) + shp[axis + 2:])


def gated_deltanet(qkv, a, b, gate, conv_w, a_log, dt_bias, norm_w):
    bsz, t, _ = qkv.shape
    c = GDN_CHUNK
    z = jax.nn.silu(causal_dwconv(qkv, conv_w)).astype(F32)
    z = z.reshape(bsz, t, 3, H_GDN, HEAD_DIM).transpose(2, 0, 3, 1, 4)
    q = l2norm(z[0]) * HEAD_DIM ** -0.5
    k = l2norm(z[1])
    v = z[2]
    beta = jax.nn.sigmoid(b.astype(F32)).transpose(0, 2, 1)
    g = (-jnp.exp(a_log.astype(F32)) * jax.nn.softplus(a.astype(F32) + dt_bias.astype(F32))).transpose(0, 2, 1)
    qc, kc, vc = (split_blocks(u, 2, c) for u in (q, k, v))
    bc = split_blocks(beta, 2, c)
    gam = jnp.cumsum(split_blocks(g, 2, c), -1)
    incl = jnp.tril(jnp.ones((c, c), bool))
    strict = jnp.tril(jnp.ones((c, c), bool), -1)
    dec = jnp.exp(jnp.where(incl, gam[..., :, None] - gam[..., None, :], -jnp.inf))
    a_mat = jnp.where(strict, bc[..., :, None] * dec * jnp.einsum('nbhrd,nbhjd->nbhrj', kc, kc), 0.0)
    rhs = jnp.concatenate([vc * bc[..., None], kc * (bc * jnp.exp(gam))[..., None]], -1)
    sol = lax.linalg.triangular_solve(a_mat + jnp.eye(c, dtype=F32), rhs, left_side=True,
                                      lower=True, unit_diagonal=True)
    u0, w = sol[..., :HEAD_DIM], sol[..., HEAD_DIM:]
    qk = dec * jnp.einsum('nbhrd,nbhjd->nbhrj', qc, kc)
    q_dec = qc * jnp.exp(gam)[..., None]
    k_dec = kc * jnp.exp(gam[..., -1:] - gam)[..., None]
    g_last = jnp.exp(gam[..., -1])

    def step(s, xs):
        u0_n, w_n, qk_n, qd_n, kd_n, gl_n = xs
        u = u0_n - jnp.einsum('bhcd,bhde->bhce', w_n, s)
        o = jnp.einsum('bhcd,bhde->bhce', qd_n, s) + jnp.einsum('bhcj,bhje->bhce', qk_n, u)
        s = s * gl_n[..., None, None] + jnp.einsum('bhcd,bhce->bhde', kd_n, u)
        return s, o

    s0 = jnp.zeros((bsz, H_GDN, HEAD_DIM, HEAD_DIM), F32)
    _, o = lax.scan(step, s0, (u0, w, qk, q_dec, k_dec, g_last))
    o = merge_blocks(o, 2).transpose(0, 2, 1, 3)
    o = rmsnorm(o, norm_w) * jax.nn.silu(gate.astype(F32).reshape(bsz, t, H_GDN, HEAD_DIM))
    return o.reshape(bsz, t, H_GDN * HEAD_DIM)


def nsa_attention(q, kv, gates, pe_k, pe_v, wk1, wk2, wv1, wv2):
    bsz, t, _ = q.shape
    q = q.astype(F32).reshape(bsz, t, G_NSA, HPG_NSA, HEAD_DIM).transpose(0, 2, 3, 1, 4) * HEAD_DIM ** -0.5
    k_c, v_c, k_s, v_s, k_w, v_w = kv.astype(F32).reshape(bsz, t, 6, G_NSA, HEAD_DIM).transpose(2, 0, 3, 1, 4)
    pos = jnp.arange(t, dtype=jnp.int32)

    n_cmp = (t - L_CMP) // S_CMP + 1
    cmp_idx = np.arange(n_cmp)[:, None] * S_CMP + np.arange(L_CMP)[None, :]

    def compress(z, pe, w1, w2):
        zb = (z[:, :, cmp_idx] + pe.astype(F32)).reshape(bsz, G_NSA, n_cmp, L_CMP * HEAD_DIM)
        return jax.nn.silu(zb @ w1.astype(F32)) @ w2.astype(F32)

    kc = compress(k_c, pe_k, wk1, wk2)
    vc = compress(v_c, pe_v, wv1, wv2)
    cmp_mask = jnp.asarray(cmp_idx[:, -1], jnp.int32)[None, :] <= pos[:, None]
    p_cmp = masked_softmax(jnp.einsum('bgptd,bgnd->bgptn', q, kc), cmp_mask)
    o_cmp = jnp.einsum('bgptn,bgnd->bgptd', p_cmp, vc)

    n_slc = t // L_SLC
    cs = np.arange(n_cmp) * S_CMP
    ss = np.arange(n_slc) * L_SLC
    overlap = ((cs[:, None] < ss[None, :] + L_SLC) & (cs[:, None] + L_CMP > ss[None, :])).astype(np.float32)
    imp = jnp.einsum('bgptn,nm->bgtm', p_cmp, jnp.asarray(overlap))
    blk = jnp.arange(n_slc, dtype=jnp.int32)[None, :]
    cur = (pos // L_SLC)[:, None]
    valid = blk <= cur
    forced = (blk == 0) | (blk == cur) | (blk == cur - 1)
    score = jnp.where(valid, jnp.where(forced, FORCE_SCORE, imp), -jnp.inf)
    n_top = min(N_SEL, n_slc)
    top_val, top_idx = lax.top_k(score, n_top)
    top_ok = jnp.isfinite(top_val)
    kb = k_s.reshape(bsz, G_NSA, n_slc, L_SLC, HEAD_DIM)
    vb = v_s.reshape(bsz, G_NSA, n_slc, L_SLC, HEAD_DIM)
    gather = jax.vmap(jax.vmap(lambda z, i: z[i]))
    offs = jnp.arange(L_SLC, dtype=jnp.int32)
    n_keys = n_top * L_SLC

    def slc_block(xs):
        qb, ib, okb, tb = xs
        ks = gather(kb, ib).reshape(bsz, G_NSA, SLC_QBLOCK, n_keys, HEAD_DIM)
        vs = gather(vb, ib).reshape(bsz, G_NSA, SLC_QBLOCK, n_keys, HEAD_DIM)
        tok = (ib[..., None] * L_SLC + offs).reshape(bsz, G_NSA, SLC_QBLOCK, n_keys)
        ok = jnp.repeat(okb, L_SLC, axis=-1) & (tok <= tb[:, None])
        pr = masked_softmax(jnp.einsum('bgpqd,bgqmd->bgpqm', qb, ks), ok[:, :, None])
        return jnp.einsum('bgpqm,bgqmd->bgpqd', pr, vs)

    o_slc = merge_blocks(lax.map(slc_block, (split_blocks(q, 3, SLC_QBLOCK),
                                             split_blocks(top_idx, 2, SLC_QBLOCK),
                                             split_blocks(top_ok, 2, SLC_QBLOCK),
                                             pos.reshape(-1, SLC_QBLOCK))), 3)

    kwp = jnp.pad(k_w, ((0, 0), (0, 0), (WINDOW, 0), (0, 0)))
    vwp = jnp.pad(v_w, ((0, 0), (0, 0), (WINDOW, 0), (0, 0)))
    span = WINDOW + ATT_QBLOCK

    def win_block(xs):
        qb, q0 = xs
        kk = lax.dynamic_slice_in_dim(kwp, q0, span, axis=2)
        vv = lax.dynamic_slice_in_dim(vwp, q0, span, axis=2)
        tq = q0 + jnp.arange(ATT_QBLOCK, dtype=jnp.int32)
        ts = q0 - WINDOW + jnp.arange(span, dtype=jnp.int32)
        d = tq[:, None] - ts[None, :]
        m = (d >= 0) & (d < WINDOW) & (ts[None, :] >= 0)
        pr = masked_softmax(jnp.einsum('bgpqd,bgkd->bgpqk', qb, kk), m)
        return jnp.einsum('bgpqk,bgkd->bgpqd', pr, vv)

    q0s = jnp.arange(t // ATT_QBLOCK, dtype=jnp.int32) * ATT_QBLOCK
    o_win = merge_blocks(lax.map(win_block, (split_blocks(q, 3, ATT_QBLOCK), q0s)), 3)

    gt = jax.nn.sigmoid(gates.astype(F32)).reshape(bsz, t, 3, G_NSA, HPG_NSA).transpose(2, 0, 3, 4, 1)[..., None]
    o = gt[0] * o_cmp + gt[1] * o_slc + gt[2] * o_win
    return o.transpose(0, 3, 1, 2, 4).reshape(bsz, t, H_NSA * HEAD_DIM)


def hgrn2(q, f, i, g, lb, norm_w):
    bsz, t, _ = q.shape
    c = HGRN_CHUNK
    shp = (bsz, t, H_HGRN, HEAD_DIM)
    lb = lb.astype(F32).reshape(H_HGRN, HEAD_DIM)
    fx = f.astype(F32).reshape(shp)
    log_f = jnp.logaddexp(jnp.log(lb), jnp.log1p(-lb) + jax.nn.log_sigmoid(fx))
    k = (1.0 - lb) * jax.nn.sigmoid(-fx)
    q = jax.nn.silu(q.astype(F32)).reshape(shp)
    v = i.astype(F32).reshape(shp)
    qc, kc, vc, lfc = (split_blocks(u.transpose(0, 2, 1, 3), 2, c) for u in (q, k, v, log_f))
    bcum = jnp.cumsum(lfc, axis=3)
    incl = jnp.tril(jnp.ones((c, c), bool))[:, :, None]

    def step(s, xs):
        q_n, k_n, v_n, b_n = xs
        dec = jnp.exp(jnp.where(incl, b_n[:, :, :, None, :] - b_n[:, :, None, :, :], -jnp.inf))
        att = jnp.einsum('bhrd,bhjd,bhrjd->bhrj', q_n, k_n, dec)
        b_last = b_n[:, :, -1]
        o = jnp.einsum('bhrd,bhde->bhre', q_n * jnp.exp(b_n), s) + jnp.einsum('bhrj,bhje->bhre', att, v_n)
        s = s * jnp.exp(b_last)[..., None] + jnp.einsum('bhjd,bhje->bhde', k_n * jnp.exp(b_last[:, :, None] - b_n), v_n)
        return s, o

    s0 = jnp.zeros((bsz, H_HGRN, HEAD_DIM, HEAD_DIM), F32)
    _, o = lax.scan(step, s0, (qc, kc, vc, bcum))
    o = merge_blocks(o, 2).transpose(0, 2, 1, 3)
    o = rmsnorm(o, norm_w) * jax.nn.silu(g.astype(F32).reshape(shp))
    return o.reshape(bsz, t, H_HGRN * HEAD_DIM)


def forgetting_attention(qkv, fgate, f_bias):
    bsz, t, _ = qkv.shape
    z = qkv.astype(F32).reshape(bsz, t, 3, H_FOX, HEAD_DIM).transpose(2, 0, 3, 1, 4)
    q, k, v = z[0] * HEAD_DIM ** -0.5, z[1], z[2]
    cum = jnp.cumsum(jax.nn.log_sigmoid(fgate.astype(F32) + f_bias.astype(F32)), axis=1).transpose(0, 2, 1)
    pos = jnp.arange(t, dtype=jnp.int32)

    def blk(xs):
        qb, cb, tq = xs
        s = jnp.einsum('bhqd,bhkd->bhqk', qb, k) + cb[..., None] - cum[:, :, None, :]
        pr = masked_softmax(s, pos[None, :] <= tq[:, None])
        return jnp.einsum('bhqk,bhkd->bhqd', pr, v)

    o = merge_blocks(lax.map(blk, (split_blocks(q, 2, ATT_QBLOCK), split_blocks(cum, 2, ATT_QBLOCK),
                                   pos.reshape(-1, ATT_QBLOCK))), 2)
    return o.transpose(0, 2, 1, 3).reshape(bsz, t, H_FOX * HEAD_DIM)


def conv_ffn(h, w_up, conv_w, conv_b, w_down):
    u = causal_dwconv(h @ w_up, conv_w) + conv_b
    gt, up = jnp.split(u, 2, axis=-1)
    return (jax.nn.silu(gt) * up) @ w_down


def setup_inputs(seed: int = 0) -> dict:
    key = jax.random.key(seed)
    ks = iter(jax.random.split(key, 40))

    def nrm(shape, scale):
        return jax.random.normal(next(ks), shape, F32) * scale

    def gain(shape):
        return 1.0 + 0.05 * jax.random.normal(next(ks), shape, F32)

    dt = jnp.exp(jax.random.uniform(next(ks), (N_AB, H_GDN), F32, np.log(1e-3), np.log(1e-1)))
    return {
        'x': nrm((BATCH, SEQ, D_MODEL), 1.0),
        'p': nrm((DEPTH, BATCH, SEQ, D_PLE), 1.0),
        'ab_norm_pre': gain((N_AB, D_MODEL)),
        'ab_norm_post': gain((N_AB, D_MODEL)),
        'ab_w_in': nrm((N_AB, D_MODEL, AB_IN), D_MODEL ** -0.5),
        'gdn_conv_w': nrm((N_AB, GDN_CONV, 3 * H_GDN * HEAD_DIM), GDN_CONV ** -0.5),
        'gdn_a_log': jnp.log(jax.random.uniform(next(ks), (N_AB, H_GDN), F32, 1.0, 16.0)),
        'gdn_dt_bias': dt + jnp.log(-jnp.expm1(-dt)),
        'gdn_norm': gain((N_AB, HEAD_DIM)),
        'nsa_pe_k': nrm((N_AB, L_CMP, HEAD_DIM), 0.1),
        'nsa_pe_v': nrm((N_AB, L_CMP, HEAD_DIM), 0.1),
        'nsa_cmp_k1': nrm((N_AB, L_CMP * HEAD_DIM, HEAD_DIM), (L_CMP * HEAD_DIM) ** -0.5),
        'nsa_cmp_k2': nrm((N_AB, HEAD_DIM, HEAD_DIM), HEAD_DIM ** -0.5),
        'nsa_cmp_v1': nrm((N_AB, L_CMP * HEAD_DIM, HEAD_DIM), (L_CMP * HEAD_DIM) ** -0.5),
        'nsa_cmp_v2': nrm((N_AB, HEAD_DIM, HEAD_DIM), HEAD_DIM ** -0.5),
        'ab_w_out': nrm((N_AB, AB_OUT, D_MODEL), AB_OUT ** -0.5),
        'cd_norm_pre': gain((N_CD, D_MODEL)),
        'cd_norm_post': gain((N_CD, D_MODEL)),
        'cd_w_in': nrm((N_CD, D_MODEL, CD_IN), D_MODEL ** -0.5),
        'hgrn_lb_logits': nrm((DEPTH, H_HGRN * HEAD_DIM), 1.0),
        'hgrn_norm': gain((N_CD, HEAD_DIM)),
        'fox_f_bias': 2.0 + nrm((N_CD, H_FOX), 0.5),
        'cd_w_out': nrm((N_CD, CD_OUT, D_MODEL), CD_OUT ** -0.5),
        'ffn_norm_pre': gain((DEPTH, D_MODEL)),
        'ffn_norm_post': gain((DEPTH, D_MODEL)),
        'ffn_w_up': nrm((DEPTH, D_MODEL, 2 * D_FF), D_MODEL ** -0.5),
        'ffn_conv_w': nrm((DEPTH, FFN_CONV, 2 * D_FF), FFN_CONV ** -0.5),
        'ffn_conv_b': nrm((DEPTH, 2 * D_FF), 0.02),
        'ffn_w_down': nrm((DEPTH, D_FF, D_MODEL), D_FF ** -0.5),
        'ple_w_proj': nrm((DEPTH, D_PLE, D_MODEL), D_PLE ** -0.5),
        'ple_gate_norm': gain((DEPTH, D_MODEL)),
        'ple_w_gate': nrm((DEPTH, D_MODEL, D_MODEL), D_MODEL ** -0.5),
        'ple_norm_post': gain((DEPTH, D_MODEL)),
    }


def reference(x, p, ab_norm_pre, ab_norm_post, ab_w_in, gdn_conv_w, gdn_a_log, gdn_dt_bias,
              gdn_norm, nsa_pe_k, nsa_pe_v, nsa_cmp_k1, nsa_cmp_k2, nsa_cmp_v1, nsa_cmp_v2,
              ab_w_out, cd_norm_pre, cd_norm_post, cd_w_in, hgrn_lb_logits, hgrn_norm,
              fox_f_bias, cd_w_out, ffn_norm_pre, ffn_norm_post, ffn_w_up, ffn_conv_w,
              ffn_conv_b, ffn_w_down, ple_w_proj, ple_gate_norm, ple_w_gate, ple_norm_post):
    sm = jax.nn.softmax(hgrn_lb_logits.astype(F32), axis=0)
    lb_table = jnp.cumsum(sm, axis=0) - sm[0]
    for li in range(DEPTH):
        j = li // 2
        if li % 2 == 0:
            h = rmsnorm(x, ab_norm_pre[j])
            g_qkv, g_a, g_b, g_gate, n_q, n_kv, n_gate = split_cols(h @ ab_w_in[j], AB_SPLITS)
            o_a = gated_deltanet(g_qkv, g_a, g_b, g_gate, gdn_conv_w[j], gdn_a_log[j],
                                 gdn_dt_bias[j], gdn_norm[j])
            o_b = nsa_attention(n_q, n_kv, n_gate, nsa_pe_k[j], nsa_pe_v[j], nsa_cmp_k1[j],
                                nsa_cmp_k2[j], nsa_cmp_v1[j], nsa_cmp_v2[j])
            y = jnp.concatenate([o_a, o_b], -1).astype(x.dtype) @ ab_w_out[j]
            x = x + rmsnorm(y, ab_norm_post[j])
        else:
            h = rmsnorm(x, cd_norm_pre[j])
            h_q, h_f, h_i, h_g, f_qkv, f_f = split_cols(h @ cd_w_in[j], CD_SPLITS)
            o_c = hgrn2(h_q, h_f, h_i, h_g, lb_table[li], hgrn_norm[j])
            o_d = forgetting_attention(f_qkv, f_f, fox_f_bias[j])
            y = jnp.concatenate([o_c, o_d], -1).astype(x.dtype) @ cd_w_out[j]
            x = x + rmsnorm(y, cd_norm_post[j])
        h = rmsnorm(x, ffn_norm_pre[li])
        x = x + rmsnorm(conv_ffn(h, ffn_w_up[li], ffn_conv_w[li], ffn_conv_b[li], ffn_w_down[li]),
                        ffn_norm_post[li])
        gate = jax.nn.sigmoid(rmsnorm(x, ple_gate_norm[li]) @ ple_w_gate[li])
        x = x + rmsnorm(gate * (p[li].astype(x.dtype) @ ple_w_proj[li]), ple_norm_post[li])
    return x
```

```python
import numpy as np
import concourse.bass as bass
import concourse.mybir as mybir
from concourse.bass_utils import run_bass_kernel_spmd

F32 = mybir.dt.float32
BF16 = mybir.dt.bfloat16
I32 = mybir.dt.int32
AF = mybir.ActivationFunctionType
ALU = mybir.AluOpType
AX = mybir.AxisListType

SEM_EPOCH = 30000
DMA_SLOTS = 8
SAME_ENGINE_SYNC = True


class H:
    __slots__ = ("w", "r", "name")

    def __init__(self, name=""):
        self.w = None
        self.r = []
        self.name = name


class _Rec:
    def __init__(self):
        self._call = None
        self._tag = None

    def __getattr__(self, name):
        def f(*a, **k):
            self._call = (name, a, k)
            return self
        return f

    def annotate(self, t):
        self._tag = t
        return self


class Prog:
    ENGS = ("pe", "act", "dve", "pool", "sp")

    def __init__(self, nc):
        self.nc = nc
        self.streams = {e: [] for e in self.ENGS}
        self.cnt = {e: 0 for e in self.ENGS}
        self.known = {e: {} for e in self.ENGS}
        self.sems = {}
        self.dma_i = {e: 0 for e in self.ENGS}
        self.dma_slot_cnt = {}
        self._n = 0

    SB_LO = 16512
    SB_HI = 229344

    def sb(self, shape, dtype, name=None):
        self._n += 1
        if not hasattr(self, "sb_off"):
            self.sb_off = self.SB_LO
        nb = int(np.prod(shape[1:])) * mybir.dt.size(dtype)
        nb = (nb + 63) // 64 * 64
        off = self.sb_off
        self.sb_off += nb
        assert self.sb_off <= self.SB_HI, f"SBUF overflow allocating {name} {shape}: {self.sb_off}"
        return self.nc.alloc_sbuf_tensor_at(f"sb{self._n}_{name or ''}", list(shape), dtype, offset=off)

    def mark(self):
        if not hasattr(self, "sb_off"):
            self.sb_off = self.SB_LO
        return self.sb_off

    def release(self, m):
        self.sb_off = m

    def barrier(self):
        toks = []
        for e in self.ENGS:
            n = self.cnt[e]
            if n > 0:
                toks.append((("e", e, n // SEM_EPOCH), n % SEM_EPOCH))
        for key, c in self.dma_slot_cnt.items():
            toks.append((key, 16 * c))
        for e in self.ENGS:
            waits = []
            for (key, val) in toks:
                if key[0] == "e" and key[1] == e:
                    continue
                if self.known[e].get(key, 0) >= val:
                    continue
                self.known[e][key] = val
                waits.append((key, val))
            if waits:
                self.streams[e].append((None, waits, None, 0))

    def ps(self, shape, dtype=F32, name=None):
        self._n += 1
        return self.nc.alloc_psum_tensor(f"ps{self._n}_{name or ""}", list(shape), dtype)

    def _sem(self, key):
        if key not in self.sems:
            self.sems[key] = self.nc.alloc_semaphore("s_" + "_".join(str(k) for k in key))
        return self.sems[key]

    def _waits(self, eng, reads, writes):
        toks = []
        for t in reads:
            if t.w is not None:
                toks.append(t.w)
        for t in writes:
            if t.w is not None:
                toks.append(t.w)
            toks.extend(t.r)
        need = {}
        kn = self.known[eng]
        for (key, val, teng) in toks:
            if teng == eng and (eng in ("pe", "sp") or not SAME_ENGINE_SYNC):
                continue
            if kn.get(key, 0) >= val:
                continue
            if need.get(key, 0) < val:
                need[key] = val
        for key, val in need.items():
            kn[key] = val
        return list(need.items())

    def op(self, eng, fn, reads=(), writes=(), tag=None):
        rec = _Rec()
        fn(rec)
        name_, a_, k_ = rec._call
        tag_ = tag or rec._tag

        def fn(e, name_=name_, a_=a_, k_=k_, tag_=tag_):
            inst = getattr(e, name_)(*a_, **k_)
            if tag_ is not None:
                inst = inst.annotate(tag_)
            return inst
        waits = self._waits(eng, reads, writes)
        self.cnt[eng] += 1
        n = self.cnt[eng]
        key = ("e", eng, n // SEM_EPOCH)
        val = n % SEM_EPOCH
        if val == 0:
            self.cnt[eng] += 1
            n = self.cnt[eng]
            key = ("e", eng, n // SEM_EPOCH)
            val = n % SEM_EPOCH
        tok = (key, val, eng)
        self.streams[eng].append((fn, waits, key, 1))
        for t in writes:
            t.w = tok
            t.r = []
        for t in reads:
            if t not in writes:
                t.r.append(tok)
        return tok

    def dma(self, q, out, in_, reads=(), writes=(), slow=False):
        i = self.dma_i[q]
        self.dma_i[q] += 1
        slot = i % DMA_SLOTS
        key = ("d", q, slot)
        c = self.dma_slot_cnt.get(key, 0) + 1
        self.dma_slot_cnt[key] = c
        val = 16 * c
        waits = self._waits(q, reads, writes)
        if c > 1 and self.known[q].get(key, 0) < val - 16:
            waits.append((key, val - 16))
            self.known[q][key] = val - 16
        tok = (key, val, "dma_" + q)

        def fn(e, out=out, in_=in_):
            if slow:
                with self.nc.allow_non_contiguous_dma(reason="tiny strided transfer"):
                    return e.dma_start(out=out, in_=in_)
            return e.dma_start(out=out, in_=in_)
        self.streams[q].append((fn, waits, key, 16))
        for t in writes:
            t.w = tok
            t.r = []
        for t in reads:
            t.r.append(tok)
        return tok

    def wait_all(self, eng, toks):
        mx = {}
        for (key, val, _) in toks:
            mx[key] = max(mx.get(key, 0), val)
        self.streams[eng].append((None, list(mx.items()), None, 0))

    def build(self):
        nc = self.nc
        for e in self.ENGS:
            for (fn, waits, key, inc) in self.streams[e]:
                if key is not None:
                    self._sem(key)
                for (k, v) in waits:
                    self._sem(k)
        emap = {"pe": "tensor", "act": "scalar", "dve": "vector", "pool": "gpsimd", "sp": "sync"}
        with nc.Block() as block:
            for e in self.ENGS:
                stream = self.streams[e]
                if not stream:
                    continue

                def body(engine, stream=stream):
                    for (fn, waits, key, inc) in stream:
                        for (k, v) in waits:
                            engine.wait_ge(self.sems[k], v)
                        if fn is not None:
                            inst = fn(engine)
                            inst.then_inc(self.sems[key], inc)
                getattr(block, emap[e])(body)
        return nc


D_MODEL = 4096
KT_D = D_MODEL // 128
EPS = 1e-6


class Ctx:
    pass


def new_prog():
    nc = bass.Bass("TRN2", target_bir_lowering=False)
    P = Prog(nc)
    return nc, P


def din(nc, name, shape, dt=F32):
    return nc.dram_tensor(name, list(shape), dt, kind="ExternalInput").ap()


def dout(nc, name, shape, dt=F32):
    return nc.dram_tensor(name, list(shape), dt, kind="ExternalOutput").ap()


def dscr(nc, name, shape, dt=F32):
    return nc.dram_tensor(name, list(shape), dt).ap()


class Pool2:
    def __init__(self, P, n, shape, dt, psum=False, name="p", pack=False):
        self.items = []
        if psum and pack:
            big = P.ps([shape[0], n * shape[1]], dt, name=name)
            for i in range(n):
                self.items.append((big[:, i * shape[1]:(i + 1) * shape[1]], H(f"{name}{i}")))
        else:
            for i in range(n):
                t = P.ps(shape, dt, name=f"{name}{i}") if psum else P.sb(shape, dt, name=f"{name}{i}")
                self.items.append((t, H(f"{name}{i}")))
        self.i = 0

    def next(self):
        it = self.items[self.i % len(self.items)]
        self.i += 1
        return it


def setup_common(P, need_ident=True):
    C = Ctx()
    return C


def rmsnorm_rows(P, C, xt, hx, gain_rep, hgain, out, hout, D=D_MODEL):
    ss, hss = C.stat.next()
    sq, hsq = C.sq.next()
    P.op("act", lambda e: e.activation(out=sq[:, :D], in_=xt, func=AF.Square, accum_out=ss[:, 0:1]),
         reads=[hx], writes=[hsq, hss])
    P.op("act", lambda e: e.activation(out=ss[:, 1:2], in_=ss[:, 0:1], func=AF.Sqrt, scale=1.0 / D, bias=C.eps[:, 0:1]),
         reads=[hss, C.heps], writes=[hss])
    P.op("dve", lambda e: e.reciprocal(out=ss[:, 2:3], in_=ss[:, 1:2]), reads=[hss], writes=[hss])
    P.op("dve", lambda e: e.scalar_tensor_tensor(out=out, in0=xt, scalar=ss[:, 2:3], in1=gain_rep,
                                                 op0=ALU.mult, op1=ALU.mult),
         reads=[hx, hss, hgain], writes=[hout])


def transpose_rows_to_hT(P, C, hb, hhb, hT, hhT, tcol0, D=D_MODEL):
    KT = D // 128
    for g in range(0, KT, 4):
        pt, hpt = C.tps.next()
        n = min(4, KT - g)
        for j in range(n):
            kt = g + j
            P.op("pe", lambda e, pt=pt, j=j, kt=kt: e.transpose(out=pt[:, j * 128:(j + 1) * 128],
                                                                in_=hb[:, kt * 128:(kt + 1) * 128], identity=C.ident[:]),
                 reads=[hhb, C.hident], writes=[hpt])
        eng = "act" if (g // 4) % 2 == 0 else "dve"
        src = pt[:, :n * 128].rearrange("p (j c) -> p j c", c=128)
        dst = hT[:, g:g + n, tcol0:tcol0 + 128]
        if eng == "act":
            P.op("act", lambda e, src=src, dst=dst: e.copy(out=dst, in_=src), reads=[hpt], writes=[hhT])
        else:
            P.op("dve", lambda e, src=src, dst=dst: e.tensor_copy(out=dst, in_=src), reads=[hpt], writes=[hhT])


def init_consts(P, C, ident_dram):
    C.ident = P.sb([128, 128], BF16, name="ident")
    C.hident = H("ident")
    P.dma("pool", C.ident[:], ident_dram, writes=[C.hident])
    C.eps = P.sb([128, 1], F32, name="eps")
    C.heps = H("eps")
    P.op("dve", lambda e: e.memset(C.eps[:], EPS), writes=[C.heps])


def gemm_stream(P, C, W, K, slabs, rhsT, hrhsT, T, emit, wq="pool", kg_max=32):
    KT = (K + 127) // 128
    Wv = W.rearrange("(kt p) n -> p kt n", p=128)
    groups = [(g, min(kg_max, KT - g)) for g in range(0, KT, kg_max)]
    NT = T // 128
    for (col0, width, mode, tag) in slabs:
        if mode == "T":
            accs = [C.acc.next() for _ in range(NT)] if len(groups) > 1 else None
        else:
            assert len(groups) == 1
        for gi, (g0, gn) in enumerate(groups):
            wb, hwb = C.wbuf.next()
            P.dma(wq, wb[:, :gn, :width], Wv[:, g0:g0 + gn, col0:col0 + width], writes=[hwb])
            if mode == "T":
                for tt in range(NT):
                    if accs is None:
                        ps, hps = C.acc.next()
                    else:
                        ps, hps = accs[tt]
                    for k in range(gn):
                        kt = g0 + k
                        P.op("pe", lambda e, ps=ps, wb=wb, k=k, kt=kt, tt=tt: e.matmul(
                            ps[:, :width], lhsT=rhsT[:, kt, tt * 128:(tt + 1) * 128], rhs=wb[:, k, :width],
                            start=(kt == 0), stop=(kt == KT - 1)), reads=[hrhsT[tt], hwb], writes=[hps])
                    if gi == len(groups) - 1:
                        emit(P, ps[:, :width], hps, mode, tag, 0, width, tt * 128, 128)
            else:
                TB = min(512, T)
                for c0 in range(0, width, 128):
                    cw = min(128, width - c0)
                    for tb in range(0, T, TB):
                        ps, hps = C.acc.next()
                        hs = [hrhsT[t] for t in range(tb // 128, (tb + TB) // 128)]
                        for k in range(gn):
                            kt = g0 + k
                            P.op("pe", lambda e, ps=ps, wb=wb, k=k, kt=kt, c0=c0, cw=cw, tb=tb: e.matmul(
                                ps[:cw, :TB], lhsT=wb[:, k, c0:c0 + cw], rhs=rhsT[:, kt, tb:tb + TB],
                                start=(kt == 0), stop=(kt == KT - 1)), reads=hs + [hwb], writes=[hps])
                        emit(P, ps[:cw, :TB], hps, mode, tag, c0, cw, tb, TB)


TOK = 1024

AB_SLABS = ([(c, 512, "F", "qkv") for c in range(0, 6144, 512)] + [(6144, 32, "F", "ab")]
            + [(c, 512, "F", "gate") for c in range(6176, 8224, 512)]
            + [(c, 512, "F", "nq") for c in range(8224, 10272, 512)]
            + [(10272 + 512 * i, 512, m, "nkv%d" % i) for i, m in enumerate("FFFTFT")]
            + [(13344, 48, "F", "ngate")])
CD_SLABS = ([(c, 512, "F", "hq") for c in range(0, 2048, 512)] + [(c, 512, "F", "hf") for c in range(2048, 4096, 512)]
            + [(c, 512, "T", "hi") for c in range(4096, 6144, 512)] + [(c, 512, "F", "hg") for c in range(6144, 8192, 512)]
            + [(c, 512, "F", "fq") for c in range(8192, 10240, 512)] + [(c, 512, "F", "fk") for c in range(10240, 12288, 512)]
            + [(c, 512, "T", "fv") for c in range(12288, 14336, 512)] + [(14336, 16, "F", "ff")])


def slab_layout(slabs):
    posF, posT = {}, {}
    nf = nt = 0
    for (c0, w, m, tag) in slabs:
        if m == "F":
            posF[c0] = nf
            nf += w
        else:
            posT[c0] = nt
            nt += w
    return nf, nt, posF, posT


def build_A(slabs, N):
    nc, P = new_prog()
    C = Ctx()
    nf, nt, posF, posT = slab_layout(slabs)
    x = din(nc, "x", [TOK, D_MODEL])
    gain = din(nc, "gain", [1, D_MODEL])
    W = din(nc, "W", [D_MODEL, N])
    ident = din(nc, "ident", [128, 128])
    outF = dout(nc, "outF", [nf, TOK])
    outT = dout(nc, "outT", [TOK, max(nt, 1)])
    init_consts(P, C, ident)
    C.stat = Pool2(P, 2, [128, 4], F32, name="stat")
    C.sq = Pool2(P, 1, [128, D_MODEL], BF16, name="sq")
    C.tps = Pool2(P, 2, [128, 512], BF16, psum=True, name="tps")
    C.acc = Pool2(P, 4, [128, 512], F32, psum=True, name="acc")
    C.wbuf = Pool2(P, 2, [128, 32, 512], BF16, name="wbuf")
    gain_rep = P.sb([128, D_MODEL], F32, name="gain_rep")
    hgain = H()
    P.dma("sp", gain_rep[:], gain.partition_broadcast(128), writes=[hgain])
    hT = P.sb([128, KT_D, TOK], BF16, name="hT")
    hhT = [H(f"hT{t}") for t in range(TOK // 128)]
    xin = Pool2(P, 2, [128, D_MODEL], F32, name="xin")
    hbp = Pool2(P, 1, [128, D_MODEL], BF16, name="hb")
    for tt in range(TOK // 128):
        xt, hx = xin.next()
        P.dma("sp", xt[:], x[tt * 128:(tt + 1) * 128, :], writes=[hx])
        hb, hhb = hbp.next()
        rmsnorm_rows(P, C, xt[:], hx, gain_rep[:], hgain, hb[:], hhb)
        transpose_rows_to_hT(P, C, hb, hhb, hT, hhT[tt], tt * 128)
    obuf = Pool2(P, 3, [128, 512], F32, name="obuf")
    toks = []
    cnt = [0]

    def emit(P, ps, hps, mode, tag, c_off, w, t0, tn, col0=[0]):
        ob, hob = obuf.next()
        pp = ps.partition_size if hasattr(ps, "partition_size") else None
        if mode == "F":
            dst = ob[:w, :tn]
        else:
            dst = ob[:tn, :w]
        if cnt[0] % 2 == 0:
            P.op("act", lambda e: e.copy(out=dst, in_=ps), reads=[hps], writes=[hob])
        else:
            P.op("dve", lambda e: e.tensor_copy(out=dst, in_=ps), reads=[hps], writes=[hob])
        cnt[0] += 1
        c0 = emit.col0
        if mode == "F":
            r0 = posF[c0] + c_off
            toks.append(P.dma("sp", outF[r0:r0 + w, t0:t0 + tn], dst, reads=[hob]))
        else:
            q0 = posT[c0]
            toks.append(P.dma("sp", outT[t0:t0 + tn, q0:q0 + w], dst, reads=[hob]))

    for s in slabs:
        emit.col0 = s[0]
        gemm_stream(P, C, W, D_MODEL, [s], hT, hhT, TOK, emit)
    P.wait_all("sp", toks)
    P.build()
    return nc


D_FF = 11008
NCH = D_FF // 128
FF_PARTS = [(0, 16), (16, 16), (32, 16), (48, 16), (64, 16), (80, 6)]
TH = TOK + 128


def norm_res_tile(P, C, y_src, x_src, g_post, hg_post, x_dst, hdst_dram, g_pre=None, hg_pre=None, hT=None, hhT=None,
                  tcol0=0, y_reads=(), x_reads=()):
    yb, hy = C.ybuf, C.hybuf
    xb, hx = C.xbuf, C.hxbuf
    hb, hhb = C.hb, C.hhb
    P.dma("sp", yb[:], y_src, reads=list(y_reads), writes=[hy])
    P.dma("sp", xb[:], x_src, reads=list(x_reads), writes=[hx])
    ss, hss = C.stat.next()
    P.op("act", lambda e: e.activation(out=hb[:], in_=yb[:], func=AF.Square, accum_out=ss[:, 0:1]),
         reads=[hy], writes=[hhb, hss])
    P.op("act", lambda e: e.activation(out=ss[:, 1:2], in_=ss[:, 0:1], func=AF.Sqrt, scale=1.0 / D_MODEL, bias=C.eps[:, 0:1]),
         reads=[hss, C.heps], writes=[hss])
    P.op("dve", lambda e: e.reciprocal(out=ss[:, 2:3], in_=ss[:, 1:2]), reads=[hss], writes=[hss])
    P.op("dve", lambda e: e.scalar_tensor_tensor(out=yb[:], in0=yb[:], scalar=ss[:, 2:3], in1=g_post[:],
                                                 op0=ALU.mult, op1=ALU.mult), reads=[hy, hss, hg_post], writes=[hy])
    P.op("pool", lambda e: e.tensor_tensor(out=xb[:], in0=xb[:], in1=yb[:], op=ALU.add), reads=[hx, hy], writes=[hx])
    tok = P.dma("sp", x_dst, xb[:], reads=[hx], writes=[hdst_dram] if hdst_dram is not None else [])
    if g_pre is not None:
        ss2, hss2 = C.stat.next()
        P.op("act", lambda e: e.activation(out=hb[:], in_=xb[:], func=AF.Square, accum_out=ss2[:, 0:1]),
             reads=[hx], writes=[hhb, hss2])
        P.op("act", lambda e: e.activation(out=ss2[:, 1:2], in_=ss2[:, 0:1], func=AF.Sqrt, scale=1.0 / D_MODEL, bias=C.eps[:, 0:1]),
             reads=[hss2, C.heps], writes=[hss2])
        P.op("dve", lambda e: e.reciprocal(out=ss2[:, 2:3], in_=ss2[:, 1:2]), reads=[hss2], writes=[hss2])
        P.op("dve", lambda e: e.scalar_tensor_tensor(out=hb[:], in0=xb[:], scalar=ss2[:, 2:3], in1=g_pre[:],
                                                     op0=ALU.mult, op1=ALU.mult), reads=[hx, hss2, hg_pre], writes=[hhb])
        transpose_rows_to_hT(P, C, hb, hhb, hT, hhT, tcol0)
    return tok


def build_C(NSUB):
    nc, P = new_prog()
    C = Ctx()
    oT = din(nc, "oT", [NSUB * D_MODEL, TH])
    xin = din(nc, "xin", [NSUB * TH, D_MODEL])
    pT = din(nc, "pT", [NSUB * 256, TOK])
    W_out = din(nc, "W_out", [D_MODEL, D_MODEL])
    import os as _os
    STOP = int(_os.environ.get("MK_STOP", "9"))
    if STOP <= 2:
        W_up = W_down = Wg = None
    else:
        W_up = din(nc, "W_up", [D_MODEL, 2 * D_FF])
        W_down = din(nc, "W_down", [D_FF, D_MODEL])
        Wg = din(nc, "Wg", [D_MODEL, D_MODEL])
    Wp = din(nc, "Wp", [256, D_MODEL])
    gains = din(nc, "gains", [5, D_MODEL])
    convw = din(nc, "convw", [128, 2 * NCH * 3])
    convb = din(nc, "convb", [128, 2 * NCH])
    ident = din(nc, "ident", [128, 128])
    xout = dout(nc, "xout", [NSUB * TOK, D_MODEL])
    yscr = dscr(nc, "yscr", [TH, D_MODEL])
    x1scr = dscr(nc, "x1scr", [TH, D_MODEL])
    x2scr = dscr(nc, "x2scr", [TOK, D_MODEL])
    y2scr = dscr(nc, "y2scr", [TOK, D_MODEL])

    init_consts(P, C, ident)
    C.stat = Pool2(P, 4, [128, 4], F32, name="stat")
    C.tps = Pool2(P, 2, [128, 512], BF16, psum=True, name="tps")
    C.acc = Pool2(P, 5, [128, 512], F32, psum=True, name="acc")
    psmall = Pool2(P, 1, [128, 16], F32, psum=True, name="psmall")
    cw = P.sb([128, 2 * NCH * 3], F32, name="convw"); hcw = H()
    cb = P.sb([128, 2 * NCH], F32, name="convb"); hcb = H()
    P.dma("sp", cw[:], convw, writes=[hcw])
    P.dma("sp", cb[:], convb, writes=[hcb])
    prevc = P.sb([128, 2 * NCH, 2], F32, name="prevc"); hprev = H()
    actT = P.sb([128, KT_D, TH], BF16, name="actT")
    hact = [H(f"act{t}") for t in range(TH // 128)]
    obuf = Pool2(P, 3, [128, 512], F32, name="obuf")
    ubuf = Pool2(P, 4, [128, 516], F32, name="ubuf")
    cacc = Pool2(P, 4, [128, 512], F32, name="cacc")
    m0 = P.mark()
    out_toks = []
    cnt = [0]

    def alloc_gemm():
        if hasattr(C, "_g"):
            C.__dict__.update(C._g)
            return
        P.release(m0)
        C.wbuf = Pool2(P, 2, [128, 32, 512], BF16, name="wbuf")
        C.aT = P.sb([128, 16, TOK], BF16, name="aT")
        C.haT = [H(f"aT{t}") for t in range(TOK // 128)]
        C.wp = P.sb([128, 2, 512], BF16, name="wp")
        C.hwp = H()
        C.pTb = P.sb([128, 2, TOK], BF16, name="pTb")
        C.hpTb = H()
        C._g = {k: getattr(C, k) for k in ("wbuf", "aT", "haT", "wp", "hwp", "pTb", "hpTb")}

    def alloc_norm():
        if hasattr(C, "_n"):
            return
        P.release(m0)
        C._n = True
        C.ybuf = P.sb([128, D_MODEL], F32, name="ybuf"); C.hybuf = H()
        C.xbuf = P.sb([128, D_MODEL], F32, name="xbuf"); C.hxbuf = H()
        C.g1 = P.sb([128, D_MODEL], F32, name="g1"); C.hg1 = H()
        C.g2 = P.sb([128, D_MODEL], F32, name="g2"); C.hg2 = H()
        C.hb = P.sb([128, D_MODEL], BF16, name="hb"); C.hhb = H()

    def evac(ps, hps, dst):
        ob, hob = obuf.next()
        if cnt[0] % 2 == 0:
            P.op("act", lambda e: e.copy(out=dst(ob), in_=ps), reads=[hps], writes=[hob])
        else:
            P.op("dve", lambda e: e.tensor_copy(out=dst(ob), in_=ps), reads=[hps], writes=[hob])
        cnt[0] += 1
        return ob, hob

    for sub in range(NSUB):
        oT_s = oT[sub * D_MODEL:(sub + 1) * D_MODEL, :]
        x_s = xin[sub * TH:(sub + 1) * TH, :]
        pT_s = pT[sub * 256:(sub + 1) * 256, :]
        xo_s = xout[sub * TOK:(sub + 1) * TOK, :]
        hy = {(t, s): H() for t in range(TH // 128) for s in range(8)}
        hx1 = [H() for _ in range(TH // 128)]
        hx2 = [H() for _ in range(TOK // 128)]
        hy2 = {(t, s): H() for t in range(TOK // 128) for s in range(8)}

        P.barrier()
        alloc_gemm()
        oTv = oT_s.rearrange("(kt p) t -> p kt t", p=128)
        for tt in range(TH // 128):
            P.dma("pool", actT[:, :, tt * 128:(tt + 1) * 128], oTv[:, :, tt * 128:(tt + 1) * 128], writes=[hact[tt]])

        def emit1(P_, ps, hps, mode, tag, c_off, w, t0, tn):
            ob, hob = evac(ps, hps, lambda ob: ob[:tn, :w])
            s = emit1.col0 // 512
            P.dma("sp", yscr[t0:t0 + tn, emit1.col0:emit1.col0 + w], ob[:tn, :w], reads=[hob], writes=[hy[(t0 // 128, s)]])
        for s in range(8):
            emit1.col0 = s * 512
            gemm_stream(P, C, W_out, D_MODEL, [(s * 512, 512, "T", "o")], actT, hact, TH, emit1)

        P.barrier()
        alloc_norm()
        P.dma("sp", C.g1[:], gains[0:1, :].partition_broadcast(128), writes=[C.hg1])
        P.dma("sp", C.g2[:], gains[1:2, :].partition_broadcast(128), writes=[C.hg2])
        for tt in range(TH // 128):
            rows = slice(tt * 128, (tt + 1) * 128)
            norm_res_tile(P, C, yscr[rows, :], x_s[rows, :], C.g1, C.hg1, x1scr[rows, :], hx1[tt],
                          g_pre=C.g2, hg_pre=C.hg2, hT=actT, hhT=hact[tt], tcol0=tt * 128,
                          y_reads=[hy[(tt, s)] for s in range(8)])

        import os as _os
        STOP = int(_os.environ.get("MK_STOP", "9"))
        if STOP <= 2:
            continue
        P.barrier()
        alloc_gemm()
        Wupv = W_up.rearrange("(kt p) n -> p kt n", p=128)
        for pi, (pc0, pn) in enumerate(FF_PARTS):
            for c2 in range(pc0, pc0 + pn, 2):
                wb, hwb = C.wbuf.next()
                P.dma("pool", wb[:, :, 0:256], Wupv[:, :, c2 * 128:c2 * 128 + 256], writes=[hwb])
                P.dma("pool", wb[:, :, 256:512], Wupv[:, :, D_FF + c2 * 128:D_FF + c2 * 128 + 256], writes=[hwb])
                for cc in range(2):
                    c = c2 + cc
                    for tb in range(2):
                        res = []
                        for half in range(2):
                            ci = half * NCH + c
                            wcol = half * 256 + cc * 128
                            ps, hps = C.acc.next()
                            hs = [hact[1 + tb * 4 + j] for j in range(4)]
                            for kt in range(KT_D):
                                P.op("pe", lambda e, ps=ps, wb=wb, kt=kt, wcol=wcol, tb=tb: e.matmul(
                                    ps[:, :], lhsT=wb[:, kt, wcol:wcol + 128], rhs=actT[:, kt, 128 + tb * 512:128 + (tb + 1) * 512],
                                    start=(kt == 0), stop=(kt == KT_D - 1)), reads=hs + [hwb], writes=[hps])
                            ub, hub = ubuf.next()
                            if tb == 0:
                                p2, hp2 = psmall.next()
                                for kt in range(KT_D):
                                    P.op("pe", lambda e, p2=p2, wb=wb, kt=kt, wcol=wcol: e.matmul(
                                        p2[:, 0:2], lhsT=wb[:, kt, wcol:wcol + 128], rhs=actT[:, kt, 126:128],
                                        start=(kt == 0), stop=(kt == KT_D - 1)), reads=[hact[0], hwb], writes=[hp2])
                                P.op("dve", lambda e, ub=ub, p2=p2: e.tensor_copy(out=ub[:, 0:2], in_=p2[:, 0:2]),
                                     reads=[hp2], writes=[hub])
                            else:
                                P.op("dve", lambda e, ub=ub, ci=ci: e.tensor_copy(out=ub[:, 0:2], in_=prevc[:, ci, :]),
                                     reads=[hprev], writes=[hub])
                            P.op("act", lambda e, ub=ub, ps=ps: e.copy(out=ub[:, 2:514], in_=ps[:, :]), reads=[hps], writes=[hub])
                            if tb == 0:
                                P.op("dve", lambda e, ub=ub, ci=ci: e.tensor_copy(out=prevc[:, ci, :], in_=ub[:, 512:514]),
                                     reads=[hub], writes=[hprev])
                            ca, hca = cacc.next()
                            P.op("dve", lambda e, ca=ca, ub=ub, ci=ci: e.tensor_scalar(
                                out=ca[:], in0=ub[:, 2:514], scalar1=cw[:, ci * 3 + 2:ci * 3 + 3], scalar2=cb[:, ci:ci + 1],
                                op0=ALU.mult, op1=ALU.add), reads=[hub, hcw, hcb], writes=[hca])
                            P.op("dve", lambda e, ca=ca, ub=ub, ci=ci: e.scalar_tensor_tensor(
                                out=ca[:], in0=ub[:, 1:513], scalar=cw[:, ci * 3 + 1:ci * 3 + 2], in1=ca[:],
                                op0=ALU.mult, op1=ALU.add), reads=[hub, hcw, hca], writes=[hca])
                            P.op("dve", lambda e, ca=ca, ub=ub, ci=ci: e.scalar_tensor_tensor(
                                out=ca[:], in0=ub[:, 0:512], scalar=cw[:, ci * 3:ci * 3 + 1], in1=ca[:],
                                op0=ALU.mult, op1=ALU.add), reads=[hub, hcw, hca], writes=[hca])
                            res.append((ca, hca))
                        (cg, hcg), (cu, hcu) = res
                        P.op("act", lambda e, cg=cg: e.activation(out=cg[:], in_=cg[:], func=AF.Silu), reads=[hcg], writes=[hcg])
                        hts = [C.haT[tb * 4 + j] for j in range(4)]
                        aT_ = C.aT
                        P.op("pool", lambda e, cg=cg, cu=cu, c=c, tb=tb, pc0=pc0, aT_=aT_: e.tensor_tensor(
                            out=aT_[:, c - pc0, tb * 512:(tb + 1) * 512], in0=cg[:], in1=cu[:], op=ALU.mult),
                            reads=[hcg, hcu], writes=hts)
            Wd_part = W_down[pc0 * 128:(pc0 + pn) * 128, :]

            def emit3(P_, ps, hps, mode, tag, c_off, w, t0, tn, pi=pi):
                s = emit3.col0 // 512
                tt = t0 // 128
                ob, hob = evac(ps, hps, lambda ob: ob[:tn, :w])
                if pi > 0:
                    pb, hpb = obuf.next()
                    P.dma("sp", pb[:tn, :w], y2scr[t0:t0 + tn, emit3.col0:emit3.col0 + w], reads=[hy2[(tt, s)]], writes=[hpb])
                    P.op("pool", lambda e: e.tensor_tensor(out=ob[:tn, :w], in0=ob[:tn, :w], in1=pb[:tn, :w], op=ALU.add),
                         reads=[hob, hpb], writes=[hob])
                P.dma("sp", y2scr[t0:t0 + tn, emit3.col0:emit3.col0 + w], ob[:tn, :w], reads=[hob], writes=[hy2[(tt, s)]])
            for s in range(8):
                emit3.col0 = s * 512
                gemm_stream(P, C, Wd_part, pn * 128, [(s * 512, 512, "T", "d")], C.aT, C.haT, TOK, emit3)

        if STOP <= 3:
            continue
        P.barrier()
        alloc_norm()
        P.dma("sp", C.g1[:], gains[2:3, :].partition_broadcast(128), writes=[C.hg1])
        P.dma("sp", C.g2[:], gains[3:4, :].partition_broadcast(128), writes=[C.hg2])
        for tt in range(TOK // 128):
            rows = slice(tt * 128, (tt + 1) * 128)
            rows1 = slice(128 + tt * 128, 128 + (tt + 1) * 128)
            norm_res_tile(P, C, y2scr[rows, :], x1scr[rows1, :], C.g1, C.hg1, x2scr[rows, :], hx2[tt],
                          g_pre=C.g2, hg_pre=C.hg2, hT=actT, hhT=hact[tt], tcol0=tt * 128,
                          y_reads=[hy2[(tt, s)] for s in range(8)], x_reads=[hx1[tt + 1]])

        if STOP <= 4:
            continue
        P.barrier()
        alloc_gemm()
        P.dma("pool", C.pTb[:], pT_s.rearrange("(kt p) t -> p kt t", p=128), writes=[C.hpTb])
        Wgv = Wg.rearrange("(kt p) n -> p kt n", p=128)
        Wpv = Wp.rearrange("(kt p) n -> p kt n", p=128)
        hz = {(t, s): H() for t in range(TOK // 128) for s in range(8)}
        for s in range(8):
            wb, hwb = C.wbuf.next()
            P.dma("pool", wb[:, :, :], Wgv[:, :, s * 512:(s + 1) * 512], writes=[hwb])
            P.dma("pool", C.wp[:, :, :], Wpv[:, :, s * 512:(s + 1) * 512], writes=[C.hwp])
            for tt in range(TOK // 128):
                ps, hps = C.acc.next()
                for kt in range(KT_D):
                    P.op("pe", lambda e, ps=ps, wb=wb, kt=kt, tt=tt: e.matmul(
                        ps[:, :], lhsT=actT[:, kt, tt * 128:(tt + 1) * 128], rhs=wb[:, kt, :],
                        start=(kt == 0), stop=(kt == KT_D - 1)), reads=[hact[tt], hwb], writes=[hps])
                ps2, hps2 = C.acc.next()
                for kt in range(2):
                    P.op("pe", lambda e, ps2=ps2, kt=kt, tt=tt, pTb_=C.pTb, wp_=C.wp: e.matmul(
                        ps2[:, :], lhsT=pTb_[:, kt, tt * 128:(tt + 1) * 128], rhs=wp_[:, kt, :],
                        start=(kt == 0), stop=(kt == 1)), reads=[C.hpTb, C.hwp], writes=[hps2])
                ob, hob = obuf.next()
                P.op("act", lambda e, ob=ob, ps=ps: e.activation(out=ob[:, :], in_=ps[:, :], func=AF.Sigmoid),
                     reads=[hps], writes=[hob])
                P.op("dve", lambda e, ob=ob, ps2=ps2: e.tensor_tensor(out=ob[:, :], in0=ob[:, :], in1=ps2[:, :], op=ALU.mult),
                     reads=[hob, hps2], writes=[hob])
                P.dma("sp", yscr[tt * 128:(tt + 1) * 128, s * 512:(s + 1) * 512], ob[:, :], reads=[hob],
                      writes=[hz[(tt, s)]])

        P.barrier()
        alloc_norm()
        P.dma("sp", C.g1[:], gains[4:5, :].partition_broadcast(128), writes=[C.hg1])
        for tt in range(TOK // 128):
            rows = slice(tt * 128, (tt + 1) * 128)
            out_toks.append(norm_res_tile(P, C, yscr[rows, :], x2scr[rows, :], C.g1, C.hg1, xo_s[rows, :], None,
                                          y_reads=[hz[(tt, s)] for s in range(8)], x_reads=[hx2[tt]]))
    P.barrier()
    P.wait_all("sp", out_toks)
    P.build()
    return nc


SEQ = 2048
NTT = SEQ // 128
HD = 128
QSCALE = HD ** -0.5


def host_consts():
    ident = np.eye(128, dtype=np.float32)
    k = np.arange(128)[:, None]
    q = np.arange(128)[None, :]
    tri = (k <= q).astype(np.float32)
    tris = (k < q).astype(np.float32)
    bd = ((k <= q) & ((k // 64) == (q // 64))).astype(np.float32)
    reset = np.ones((128, SEQ), np.float32)
    reset[:, ::64] = 0.0
    ones = np.ones((128, 128), np.float32)
    return dict(ident=ident, tri=tri, tris=tris, bd=bd, reset=reset, ones=ones)


def load_consts(P, C, nc, names):
    if not hasattr(C, "cd"):
        C.cd = {}
    for nm, shape, dt in names:
        ap = din(nc, "c_" + nm, shape)
        t = P.sb(shape, dt, name="c_" + nm)
        h = H("c_" + nm)
        P.dma("pool", t[:], ap, writes=[h])
        C.cd[nm] = (t, h)


def fox_section(P, C, nc, fqT, fkT, fv, ffT, fbias, oT, row0, out_toks):
    ident_f, hidf = C.cd["identf"]
    ones_b, hones = C.cd["ones"]
    tri_b, htri = C.cd["tri"]
    NH = 8
    m_sec = P.mark()
    ff = P.sb([NH, SEQ], F32, name="ff"); hff = H()
    fb = P.sb([NH, 1], F32, name="fb"); hfb = H()
    cum = P.sb([NH, SEQ], F32, name="cum"); hcum = H()
    one8 = P.sb([NH, SEQ], F32, name="one8"); hone8 = H()
    P.dma("sp", ff[:], ffT, writes=[hff])
    P.dma("sp", fb[:], fbias, writes=[hfb])
    P.op("dve", lambda e: e.memset(one8[:], 1.0), writes=[hone8])
    P.op("act", lambda e: e.activation(out=ff[:], in_=ff[:], func=AF.Sigmoid, bias=fb[:, 0:1], scale=1.0),
         reads=[hff, hfb], writes=[hff])
    P.op("act", lambda e: e.activation(out=ff[:], in_=ff[:], func=AF.Ln), reads=[hff], writes=[hff])
    P.op("dve", lambda e: e.tensor_tensor_scan(out=cum[:], data0=one8[:], data1=ff[:], initial=0.0,
                                               op0=ALU.mult, op1=ALU.add), reads=[hone8, hff], writes=[hcum])
    pst, hpst = C.pmisc.next()
    for tt in range(NTT):
        P.op("pe", lambda e, tt=tt: e.transpose(out=pst[:, tt * NH:(tt + 1) * NH], in_=cum[0:NH, tt * 128:(tt + 1) * 128],
                                                identity=ident_f[0:NH, 0:NH]), reads=[hcum, hidf], writes=[hpst])
    cumcol = P.sb([128, NTT * NH], F32, name="cumcol"); hcc = H()
    P.op("dve", lambda e: e.tensor_copy(out=cumcol[:], in_=pst[:, 0:NTT * NH]), reads=[hpst], writes=[hcc])
    scr = dscr(nc, "fox_cref_scr", [NH, NTT])
    hscr = H()
    P.dma("sp", scr, cum[0:NH, :].rearrange("p (tt c) -> p tt c", c=128)[:, :, 64], reads=[hcum], writes=[hscr], slow=True)
    cref = P.sb([128, NH * NTT], F32, name="cref"); hcref = H()
    P.dma("sp", cref[:], scr.rearrange("h t -> (h t)").partition_broadcast(128), reads=[hscr], writes=[hcref])
    qTp = Pool2(P, 2, [128, SEQ], BF16, name="fq")
    kTp = Pool2(P, 2, [128, SEQ], BF16, name="fk")
    vp = Pool2(P, 2, [128, NTT, 128], BF16, name="fv")
    btp = Pool2(P, 2, [128, NTT * NTT], F32, name="fbt")
    ptp = Pool2(P, 4, [128, 128], BF16, name="fpt")
    otp = Pool2(P, 3, [128, 128], F32, name="fot")
    rdp = Pool2(P, 2, [128, 128], F32, name="frd")
    for h in range(NH):
        qT, hq = qTp.next(); kT, hk = kTp.next(); v, hv = vp.next(); bt, hbt = btp.next()
        P.dma("pool", qT[:], fqT[h * 128:(h + 1) * 128, :], writes=[hq])
        P.dma("pool", kT[:], fkT[h * 128:(h + 1) * 128, :], writes=[hk])
        P.dma("pool", v[:], fv[:, h * 128:(h + 1) * 128].rearrange("(tt p) d -> p tt d", p=128), writes=[hv])
        for kt in range(NTT):
            P.op("dve", lambda e, kt=kt, bt=bt, h=h: e.tensor_scalar(
                out=bt[:, kt * NTT:(kt + 1) * NTT], in0=cref[:, h * NTT:(h + 1) * NTT],
                scalar1=cumcol[:, kt * NH + h:kt * NH + h + 1], scalar2=None, op0=ALU.subtract),
                reads=[hcref, hcc], writes=[hbt])
        for qt in range(NTT):
            num, hnum = C.pnum.next()
            den, hden = C.pden.next()
            for kt in range(qt + 1):
                S, hS = C.psc.next()
                P.op("pe", lambda e, S=S, kT=kT, qT=qT, kt=kt, qt=qt: e.matmul(
                    S[:, :], lhsT=kT[:, kt * 128:(kt + 1) * 128], rhs=qT[:, qt * 128:(qt + 1) * 128], start=True, stop=True),
                    reads=[hk, hq], writes=[hS])
                PT, hPT = ptp.next()
                P.op("act", lambda e, S=S, PT=PT, bt=bt, kt=kt, qt=qt: e.activation(
                    out=PT[:], in_=S[:, :], func=AF.Exp, bias=bt[:, kt * NTT + qt:kt * NTT + qt + 1], scale=QSCALE),
                    reads=[hS, hbt], writes=[hPT])
                if kt == qt:
                    P.op("dve", lambda e, PT=PT: e.tensor_tensor(out=PT[:], in0=PT[:], in1=tri_b[:], op=ALU.mult),
                         reads=[hPT, htri], writes=[hPT])
                P.op("pe", lambda e, num=num, v=v, PT=PT, kt=kt, qt=qt: e.matmul(
                    num[:, :], lhsT=v[:, kt, :], rhs=PT[:], start=(kt == 0), stop=(kt == qt)), reads=[hv, hPT], writes=[hnum])
                P.op("pe", lambda e, den=den, PT=PT, kt=kt, qt=qt: e.matmul(
                    den[:, :], lhsT=ones_b[:], rhs=PT[:], start=(kt == 0), stop=(kt == qt)), reads=[hones, hPT], writes=[hden])
            rd, hrd = rdp.next()
            ot, hot = otp.next()
            P.op("dve", lambda e, rd=rd, den=den: e.reciprocal(out=rd[:], in_=den[:, :]), reads=[hden], writes=[hrd])
            P.op("dve", lambda e, ot=ot, num=num, rd=rd: e.tensor_tensor(out=ot[:], in0=num[:, :], in1=rd[:], op=ALU.mult),
                 reads=[hnum, hrd], writes=[hot])
            out_toks.append(P.dma("sp", oT[row0 + h * 128:row0 + (h + 1) * 128, qt * 128:(qt + 1) * 128], ot[:], reads=[hot]))
    P.barrier()
    P.release(m_sec)


def gated_norm_out(P, C, otile, hot, sg_ap, hsg, normw_ap, hnw, dst_dram, out_toks, n=128):
    ones_b, hones = C.cd["ones"]
    sq, hsq = C.sqp.next()
    P.op("pool", lambda e: e.tensor_tensor(out=sq[:, :n], in0=otile[:, :n], in1=otile[:, :n], op=ALU.mult),
         reads=[hot], writes=[hsq])
    ps, hps = C.pmisc.next()
    P.op("pe", lambda e: e.matmul(ps[:, :n], lhsT=ones_b[:], rhs=sq[:, :n], start=True, stop=True),
         reads=[hones, hsq], writes=[hps])
    rs, hrs = C.rsp.next()
    P.op("act", lambda e: e.activation(out=rs[:, :n], in_=ps[:, :n], func=AF.Sqrt, scale=1.0 / HD, bias=C.eps[:, 0:1]),
         reads=[hps, C.heps], writes=[hrs])
    P.op("dve", lambda e: e.reciprocal(out=rs[:, :n], in_=rs[:, :n]), reads=[hrs], writes=[hrs])
    P.op("dve", lambda e: e.tensor_tensor(out=otile[:, :n], in0=otile[:, :n], in1=rs[:, :n], op=ALU.mult),
         reads=[hot, hrs], writes=[hot])
    P.op("dve", lambda e: e.scalar_tensor_tensor(out=otile[:, :n], in0=otile[:, :n], scalar=normw_ap, in1=sg_ap,
                                                 op0=ALU.mult, op1=ALU.mult), reads=[hot, hnw, hsg], writes=[hot])
    out_toks.append(P.dma("sp", dst_dram, otile[:, :n], reads=[hot]))


def hgrn_section(P, C, nc, hqT, hfT, hgT, hi, lbl, hnorm, oT, row0, out_toks):
    ident_b, hidb = C.cd["ident"]
    bd_b, hbd = C.cd["bd"]
    reset, hreset = C.cd["reset"]
    NH = 8
    NCK = SEQ // 64
    m_sec = P.mark()
    lb = P.sb([128, 16], F32, name="lbl"); hlb = H()
    P.dma("sp", lb[:], lbl, writes=[hlb])
    nw = P.sb([128, 1], F32, name="hnw"); hnw = H()
    P.dma("sp", nw[:], hnorm, writes=[hnw])
    lbv = P.sb([128, 16], F32, name="lbv"); hlbv = H()
    P.op("dve", lambda e: e.tensor_tensor(out=lbv[:, 0:8], in0=lb[:, 8:16], in1=lb[:, 0:8], op=ALU.subtract), reads=[hlb], writes=[hlbv])
    P.op("act", lambda e: e.activation(out=lbv[:, 0:8], in_=lbv[:, 0:8], func=AF.Sigmoid), reads=[hlbv], writes=[hlbv])
    P.op("dve", lambda e: e.tensor_scalar(out=lbv[:, 8:16], in0=lbv[:, 0:8], scalar1=-1.0, scalar2=1.0, op0=ALU.mult, op1=ALU.add),
         reads=[hlbv], writes=[hlbv])
    tA = P.sb([128, SEQ], F32, name="tA"); hA = H()
    tB = P.sb([128, SEQ], F32, name="tB"); hB = H()
    tC = P.sb([128, SEQ], F32, name="tC"); hC = H()
    tD = P.sb([128, SEQ], F32, name="tD"); hD = H()
    tE = P.sb([128, SEQ], F32, name="tE"); hE = H()
    tF = P.sb([128, SEQ], F32, name="tF"); hF = H()
    NB = 2
    qtl = Pool2(P, NB, [128, SEQ], BF16, name="hqt")
    ktl = Pool2(P, NB, [128, SEQ], BF16, name="hkt")
    qdc = Pool2(P, NB, [128, SEQ], BF16, name="hqd")
    khT = Pool2(P, 1, [128, SEQ], BF16, name="hkhT")
    khk = Pool2(P, NB, [128, NTT, 128], BF16, name="hkh")
    vtk = Pool2(P, NB, [128, NTT, 128], BF16, name="hv")
    sgp = Pool2(P, NB, [128, SEQ], F32, name="hsg")
    glp = Pool2(P, NB, [128, NCK], F32, name="hgl")
    Sf = Pool2(P, 2, [128, 128], F32, name="hSf")
    Sb = Pool2(P, 2, [128, 128], BF16, name="hSb")
    atp = Pool2(P, 2, [128, 128], BF16, name="hat")
    otp = Pool2(P, 3, [128, 128], F32, name="hot")

    def pre(h):
        qt_, hqt = qtl.next(); kt_, hkt = ktl.next(); qd_, hqd = qdc.next(); kh_, hkh = khk.next()
        v_, hv = vtk.next(); sg_, hsg = sgp.next(); gl_, hgl = glp.next(); khT_, hkhT = khT.next()
        rows = slice(h * 128, (h + 1) * 128)
        P.dma("sp", tA[:], hfT[rows, :], writes=[hA])
        P.dma("sp", tB[:], hqT[rows, :], writes=[hB])
        P.dma("sp", sg_[:], hgT[rows, :], writes=[hsg])
        P.dma("pool", v_[:], hi[:, rows].rearrange("(tt p) d -> p tt d", p=128), writes=[hv])
        P.op("act", lambda e: e.activation(out=tA[:], in_=tA[:], func=AF.Sigmoid), reads=[hA], writes=[hA])
        P.op("dve", lambda e: e.tensor_scalar(out=tA[:], in0=tA[:], scalar1=lbv[:, 8 + h:9 + h], scalar2=lbv[:, h:h + 1],
                                              op0=ALU.mult, op1=ALU.add), reads=[hA, hlbv], writes=[hA])
        P.op("act", lambda e: e.activation(out=tC[:], in_=tA[:], func=AF.Ln), reads=[hA], writes=[hC])
        P.op("pool", lambda e: e.tensor_scalar(out=tA[:], in0=tA[:], scalar1=-1.0, scalar2=1.0, op0=ALU.mult, op1=ALU.add),
             reads=[hA], writes=[hA])
        P.op("dve", lambda e: e.tensor_tensor_scan(out=tD[:], data0=reset[:], data1=tC[:], initial=0.0,
                                                   op0=ALU.mult, op1=ALU.add), reads=[hreset, hC], writes=[hD])
        b3 = tD[:].rearrange("p (n c) -> p n c", c=64)
        bm = b3[:, :, 31:32].broadcast_to([128, NCK, 64])
        bl = b3[:, :, 63:64].broadcast_to([128, NCK, 64])
        c3 = tC[:].rearrange("p (n c) -> p n c", c=64)
        e3 = tE[:].rearrange("p (n c) -> p n c", c=64)
        P.op("act", lambda e: e.activation(out=tB[:], in_=tB[:], func=AF.Silu), reads=[hB], writes=[hB])
        P.op("act", lambda e: e.activation(out=sg_[:], in_=sg_[:], func=AF.Silu), reads=[hsg], writes=[hsg])
        P.op("dve", lambda e: e.tensor_tensor(out=c3, in0=b3, in1=bm, op=ALU.subtract), reads=[hD], writes=[hC])
        P.op("act", lambda e: e.activation(out=tE[:], in_=tC[:], func=AF.Exp), reads=[hC], writes=[hE])
        P.op("pool", lambda e: e.tensor_tensor(out=qt_[:], in0=tB[:], in1=tE[:], op=ALU.mult), reads=[hB, hE], writes=[hqt])
        P.op("act", lambda e: e.activation(out=tF[:], in_=tC[:], func=AF.Exp, scale=-1.0), reads=[hC], writes=[hF])
        P.op("dve", lambda e: e.tensor_tensor(out=kt_[:], in0=tA[:], in1=tF[:], op=ALU.mult), reads=[hA, hF], writes=[hkt])
        P.op("act", lambda e: e.activation(out=tE[:], in_=tD[:], func=AF.Exp), reads=[hD], writes=[hE])
        P.op("pool", lambda e: e.tensor_tensor(out=qd_[:], in0=tB[:], in1=tE[:], op=ALU.mult), reads=[hB, hE], writes=[hqd])
        P.op("dve", lambda e: e.tensor_tensor(out=c3, in0=b3, in1=bl, op=ALU.subtract), reads=[hD], writes=[hC])
        P.op("act", lambda e: e.activation(out=tF[:], in_=tC[:], func=AF.Exp, scale=-1.0), reads=[hC], writes=[hF])
        P.op("dve", lambda e: e.tensor_tensor(out=khT_[:], in0=tA[:], in1=tF[:], op=ALU.mult), reads=[hA, hF], writes=[hkhT])
        P.op("act", lambda e: e.activation(out=gl_[:], in_=b3[:, :, 63], func=AF.Exp), reads=[hD], writes=[hgl])
        for g in range(0, NTT, 4):
            pt, hpt = C.tps.next()
            for j in range(4):
                P.op("pe", lambda e, pt=pt, j=j, g=g: e.transpose(out=pt[:, j * 128:(j + 1) * 128],
                                                                  in_=khT_[:, (g + j) * 128:(g + j + 1) * 128], identity=ident_b[:]),
                     reads=[hkhT, hidb], writes=[hpt])
            P.op("act", lambda e, pt=pt, g=g: e.copy(out=kh_[:, g:g + 4, :], in_=pt[:, :].rearrange("p (j c) -> p j c", c=128)),
                 reads=[hpt], writes=[hkh])
        return dict(qt=(qt_, hqt), kt=(kt_, hkt), qd=(qd_, hqd), kh=(kh_, hkh), v=(v_, hv), sg=(sg_, hsg), gl=(gl_, hgl), h=h)

    def loop(D):
        (qt_, hqt), (kt_, hkt), (qd_, hqd), (kh_, hkh), (v_, hv), (sg_, hsg), (gl_, hgl) = (
            D["qt"], D["kt"], D["qd"], D["kh"], D["v"], D["sg"], D["gl"])
        h = D["h"]
        S, hS = Sf.next(); Sbf, hSb = Sb.next()
        P.op("dve", lambda e: e.memset(S[:], 0.0), writes=[hS])
        P.op("pool", lambda e: e.memset(Sbf[:], 0.0), writes=[hSb])
        for tt in range(NTT):
            cols = slice(tt * 128, (tt + 1) * 128)
            aps, haps = C.psc.next()
            P.op("pe", lambda e, aps=aps, cols=cols: e.matmul(aps[:, :], lhsT=kt_[:, cols], rhs=qt_[:, cols], start=True, stop=True),
                 reads=[hkt, hqt], writes=[haps])
            at, hat = atp.next()
            P.op("dve", lambda e, at=at, aps=aps: e.tensor_tensor(out=at[:], in0=aps[:, :], in1=bd_b[:], op=ALU.mult),
                 reads=[haps, hbd], writes=[hat])
            ot, hot = otp.next()
            for half in range(2):
                n = tt * 2 + half
                pr = slice(half * 64, (half + 1) * 64)
                c64 = slice(n * 64, (n + 1) * 64)
                ops, hops = C.pnum.next()
                P.op("pe", lambda e, ops=ops, Sbf=Sbf, c64=c64: e.matmul(ops[:, 0:64], lhsT=Sbf[:], rhs=qd_[:, c64], start=True, stop=False),
                     reads=[hSb, hqd], writes=[hops])
                P.op("pe", lambda e, ops=ops, at=at, pr=pr, tt=tt: e.matmul(ops[:, 0:64], lhsT=v_[pr, tt, :], rhs=at[pr, pr], start=False, stop=True),
                     reads=[hv, hat], writes=[hops])
                P.op("act", lambda e, ot=ot, ops=ops, pr=pr: e.copy(out=ot[:, pr], in_=ops[:, 0:64]), reads=[hops], writes=[hot])
                kv, hkv = C.pden.next()
                P.op("pe", lambda e, kv=kv, pr=pr, tt=tt: e.matmul(kv[:, :], lhsT=kh_[pr, tt, :], rhs=v_[pr, tt, :], start=True, stop=True),
                     reads=[hkh, hv], writes=[hkv])
                S2, hS2 = Sf.next(); Sbf2, hSb2 = Sb.next()
                P.op("dve", lambda e, S2=S2, S=S, kv=kv, n=n: e.scalar_tensor_tensor(
                    out=S2[:], in0=S[:], scalar=gl_[:, n:n + 1], in1=kv[:, :], op0=ALU.mult, op1=ALU.add),
                    reads=[hS, hgl, hkv], writes=[hS2])
                P.op("act", lambda e, Sbf2=Sbf2, S2=S2: e.copy(out=Sbf2[:], in_=S2[:]), reads=[hS2], writes=[hSb2])
                S, hS, Sbf, hSb = S2, hS2, Sbf2, hSb2
            gated_norm_out(P, C, ot, hot, sg_[:, cols], hsg, nw[:, 0:1], hnw,
                           oT[row0 + h * 128:row0 + (h + 1) * 128, cols], out_toks)

    Dn = pre(0)
    for h in range(NH):
        Dc = Dn
        if h + 1 < NH:
            Dn = pre(h + 1)
        loop(Dc)
    P.barrier()
    P.release(m_sec)


M1_CONSTS = [("ident", [128, 128], BF16), ("identf", [128, 128], F32), ("ones", [128, 128], BF16), ("tri", [128, 128], BF16),
             ("bd", [128, 128], BF16), ("reset", [128, SEQ], F32)]


def mixer_pools(P, C):
    C.tps = Pool2(P, 1, [128, 512], BF16, psum=True, name="tps")
    C.psc = Pool2(P, 2, [128, 128], F32, psum=True, name="psc")
    C.pnum = Pool2(P, 2, [128, 128], F32, psum=True, name="pnum")
    C.pden = Pool2(P, 2, [128, 128], F32, psum=True, name="pden")
    C.pmisc = Pool2(P, 1, [128, 128], F32, psum=True, name="pmisc")
    C.sqp = Pool2(P, 2, [128, 128], BF16, name="sqp")
    C.rsp = Pool2(P, 2, [128, 128], F32, name="rsp")
    C.eps = P.sb([128, 1], F32, name="eps")
    C.heps = H("eps")
    P.op("dve", lambda e: e.memset(C.eps[:], EPS), writes=[C.heps])


def build_M1(do_fox=True, do_hgrn=True):
    nc, P = new_prog()
    C = Ctx()
    fqT = din(nc, "fqT", [1024, SEQ]); fkT = din(nc, "fkT", [1024, SEQ]); fv = din(nc, "fv", [SEQ, 1024])
    ffT = din(nc, "ffT", [8, SEQ]); fbias = din(nc, "fbias", [8, 1])
    hqT = din(nc, "hqT", [1024, SEQ]); hfT = din(nc, "hfT", [1024, SEQ]); hgT = din(nc, "hgT", [1024, SEQ])
    hi = din(nc, "hi", [SEQ, 1024]); lbl = din(nc, "lbl", [128, 16]); hnorm = din(nc, "hnorm", [128, 1])
    oT = dout(nc, "oT", [2048, SEQ])
    load_consts(P, C, nc, M1_CONSTS)
    mixer_pools(P, C)
    out_toks = []
    if do_hgrn:
        hgrn_section(P, C, nc, hqT, hfT, hgT, hi, lbl, hnorm, oT, 0, out_toks)
    if do_fox:
        fox_section(P, C, nc, fqT, fkT, fv, ffT, fbias, oT, 1024, out_toks)
    P.wait_all("sp", out_toks)
    P.build()
    return nc


def host_consts0():
    hc = host_consts()
    r = np.arange(128)[:, None]
    j = np.arange(128)[None, :]
    hc["negsl"] = -(j < r).astype(np.float32)
    reset128 = np.ones((8, SEQ), np.float32)
    reset128[:, ::128] = 0.0
    hc["reset128"] = reset128
    hc["one8"] = np.ones((8, 1), np.float32)
    return hc


def mixer_pools0(P, C):
    C.bk = [(P.ps([128, 512], F32, name=f"bk{i}"), H(f"bk{i}")) for i in range(7)]
    C.tpsb = Pool2(P, 1, [128, 512], BF16, psum=True, name="tpsb")
    C.pmisc = Pool2(P, 1, [128, 128], F32, psum=False, name="dummy")
    C.sqp = Pool2(P, 2, [128, 128], BF16, name="sqp")
    C.rsp = Pool2(P, 2, [128, 128], F32, name="rsp")
    C.eps = P.sb([128, 1], F32, name="eps")
    C.heps = H("eps")
    P.op("dve", lambda e: e.memset(C.eps[:], EPS), writes=[C.heps])


class Rot:
    def __init__(self, items):
        self.items = items
        self.i = 0

    def next(self):
        it = self.items[self.i % len(self.items)]
        self.i += 1
        return it


def gdn_section(P, C, nc, gqkvT, gconvw, gaT, gbT, galog, gdtb, ggT, gnorm, oT, row0, out_toks):
    ident_b, hidb = C.cd["ident"]
    ident_f, hidf = C.cd["identf"]
    ones_b, hones = C.cd["ones"]
    negsl, hnegsl = C.cd["negsl"]
    triu, htriu = C.cd["trif"]
    reset128, hr128 = C.cd["reset128"]
    one8, hone8 = C.cd["one8"]
    NH = 8
    NC_ = SEQ // 128
    m_sec = P.mark()
    bkA = Rot(C.bk[0:3])
    bkB = Rot(C.bk[3:5])
    bkS = Rot(C.bk[5:7])
    C.pmisc = Rot([C.bk[4]])
    ra = P.sb([NH, SEQ], F32, name="ra"); hra = H()
    rb = P.sb([NH, SEQ], F32, name="rb"); hrb = H()
    gam = P.sb([NH, SEQ], F32, name="gam"); hgam = H()
    sc8 = P.sb([NH, 4], F32, name="sc8"); hsc8 = H()
    P.dma("sp", ra[:], gaT, writes=[hra])
    P.dma("sp", rb[:], gbT, writes=[hrb])
    P.dma("sp", sc8[:, 0:1], galog, writes=[hsc8])
    P.dma("sp", sc8[:, 1:2], gdtb, writes=[hsc8])
    P.op("act", lambda e: e.activation(out=sc8[:, 2:3], in_=sc8[:, 0:1], func=AF.Exp), reads=[hsc8], writes=[hsc8])
    P.op("dve", lambda e: e.tensor_scalar(out=sc8[:, 2:3], in0=sc8[:, 2:3], scalar1=-1.0, scalar2=None, op0=ALU.mult),
         reads=[hsc8], writes=[hsc8])
    P.op("act", lambda e: e.activation(out=ra[:], in_=ra[:], func=AF.Exp, bias=sc8[:, 1:2], scale=1.0), reads=[hra, hsc8], writes=[hra])
    P.op("act", lambda e: e.activation(out=ra[:], in_=ra[:], func=AF.Ln, bias=one8[:, 0:1], scale=1.0), reads=[hra, hone8], writes=[hra])
    P.op("dve", lambda e: e.tensor_scalar(out=ra[:], in0=ra[:], scalar1=sc8[:, 2:3], scalar2=None, op0=ALU.mult),
         reads=[hra, hsc8], writes=[hra])
    P.op("dve", lambda e: e.tensor_tensor_scan(out=gam[:], data0=reset128[:], data1=ra[:], initial=0.0, op0=ALU.mult, op1=ALU.add),
         reads=[hr128, hra], writes=[hgam])
    P.op("act", lambda e: e.activation(out=rb[:], in_=rb[:], func=AF.Sigmoid), reads=[hrb], writes=[hrb])
    gscr = dscr(nc, "gdn_gam_scr", [NH, SEQ]); hgscr = H()
    P.dma("sp", gscr, gam[:], reads=[hgam], writes=[hgscr])
    lscr = dscr(nc, "gdn_gl_scr", [NH, NC_]); hlscr = H()
    P.dma("sp", lscr, gam[0:NH, :].rearrange("p (n c) -> p n c", c=128)[:, :, 127], reads=[hgam], writes=[hlscr], slow=True)
    glrep = P.sb([128, NH * NC_], F32, name="glrep"); hglrep = H()
    P.dma("sp", glrep[:], lscr.rearrange("h n -> (h n)").partition_broadcast(128), reads=[hlscr], writes=[hglrep])
    gcol = P.sb([128, NC_ * NH], F32, name="gcol"); hgcol = H()
    bcol = P.sb([128, NC_ * NH], F32, name="bcol"); hbcol = H()
    for src, hsrc, dst, hdst in ((gam, hgam, gcol, hgcol), (rb, hrb, bcol, hbcol)):
        pst, hpst = bkB.next()
        for tt in range(NC_):
            P.op("pe", lambda e, tt=tt, src=src, pst=pst: e.transpose(out=pst[:, tt * NH:(tt + 1) * NH], in_=src[0:NH, tt * 128:(tt + 1) * 128],
                                                                      identity=ident_f[0:NH, 0:NH]), reads=[hsrc, hidf], writes=[hpst])
        P.op("dve", lambda e, dst=dst, pst=pst: e.tensor_copy(out=dst[:], in_=pst[:, 0:NC_ * NH]), reads=[hpst], writes=[hdst])
    wcol = P.sb([128, NC_ * NH], F32, name="wcol"); hwcol = H()
    kdcol = P.sb([128, NC_ * NH], F32, name="kdcol"); hkdcol = H()
    elast = P.sb([128, NH * NC_], F32, name="elast"); helast = H()
    P.op("act", lambda e: e.activation(out=wcol[:], in_=gcol[:], func=AF.Exp), reads=[hgcol], writes=[hwcol])
    P.op("dve", lambda e: e.tensor_tensor(out=wcol[:], in0=wcol[:], in1=bcol[:], op=ALU.mult), reads=[hwcol, hbcol], writes=[hwcol])
    P.op("dve", lambda e: e.tensor_tensor(out=kdcol[:].rearrange("p (n h) -> p n h", h=NH),
                                          in0=glrep[:].rearrange("p (h n) -> p n h", n=NC_),
                                          in1=gcol[:].rearrange("p (n h) -> p n h", h=NH), op=ALU.subtract),
         reads=[hglrep, hgcol], writes=[hkdcol])
    P.op("act", lambda e: e.activation(out=kdcol[:], in_=kdcol[:], func=AF.Exp), reads=[hkdcol], writes=[hkdcol])
    P.op("act", lambda e: e.activation(out=elast[:], in_=glrep[:], func=AF.Exp), reads=[hglrep], writes=[helast])
    cw = P.sb([128, 24 * 4], F32, name="gcw"); hcw = H()
    P.dma("sp", cw[:], gconvw, writes=[hcw])
    nw = P.sb([128, 1], F32, name="gnw"); hnw = H()
    P.dma("sp", nw[:], gnorm, writes=[hnw])
    tX = [(P.sb([128, SEQ], F32, name=f"gtx{i}"), H()) for i in range(2)]
    tY = (P.sb([128, SEQ], F32, name="gty"), H())
    NB = 2
    qTp = Pool2(P, NB, [128, SEQ], BF16, name="gq"); kTp = Pool2(P, NB, [128, SEQ], BF16, name="gk")
    vTp = Pool2(P, 1, [128, SEQ], BF16, name="gv"); qdp = Pool2(P, NB, [128, SEQ], BF16, name="gqd")
    Gp = Pool2(P, NB, [128, SEQ], F32, name="gG"); sgp = Pool2(P, NB, [128, SEQ], F32, name="gsg")
    kbgp = Pool2(P, NB, [128, NC_, 128], BF16, name="gkbg"); kdp = Pool2(P, NB, [128, NC_, 128], BF16, name="gkd")
    vbp = Pool2(P, NB, [128, NC_, 128], BF16, name="gvb")
    f4 = lambda nm, n=1: Pool2(P, n, [128, 4, 128], F32, name=nm)
    d0p, decp, decTp, tmpp = f4("gd0"), f4("gdec"), f4("gdecT"), f4("gtmp")
    Yp, YTp, P4p = f4("gY", 2), f4("gYT", 2), f4("gP4")
    u0p = f4("gu0", 2)
    b4 = lambda nm, n=1: Pool2(P, n, [128, 4, 128], BF16, name=nm)
    TTp, qkTp, wTp = b4("gTT"), b4("gqkT", 2), b4("gwT", 2)
    up = Pool2(P, 2, [128, 128], BF16, name="gu")
    Sf = Pool2(P, 2, [128, 128], F32, name="gSf"); Sb = Pool2(P, 2, [128, 128], BF16, name="gSb")
    otp = Pool2(P, 3, [128, 128], F32, name="got")
    evc = [0]

    def evac(dst, src, reads, writes, tag=None):
        if evc[0] % 2 == 0:
            P.op("act", lambda e: e.copy(out=dst, in_=src), reads=reads, writes=writes, tag=tag)
        else:
            P.op("dve", lambda e: e.tensor_copy(out=dst, in_=src), reads=reads, writes=writes, tag=tag)
        evc[0] += 1

    def conv_silu(h, s, dst_f32, hdst):
        x, hx = tX[s % 2]
        ci = (s * NH + h) * 4
        P.dma("sp", x[:], gqkvT[(s * NH + h) * 128:(s * NH + h + 1) * 128, :], writes=[hx])
        y = dst_f32
        P.op("dve", lambda e: e.tensor_scalar(out=y[:], in0=x[:], scalar1=cw[:, ci + 3:ci + 4], scalar2=None, op0=ALU.mult),
             reads=[hx, hcw], writes=[hdst])
        for sh in (1, 2, 3):
            P.op("dve", lambda e, sh=sh: e.scalar_tensor_tensor(out=y[:, sh:], in0=x[:, :SEQ - sh], scalar=cw[:, ci + 3 - sh:ci + 4 - sh],
                                                                in1=y[:, sh:], op0=ALU.mult, op1=ALU.add),
                 reads=[hx, hcw, hdst], writes=[hdst])
        P.op("act", lambda e: e.activation(out=y[:], in_=y[:], func=AF.Silu), reads=[hdst], writes=[hdst])

    def l2norm_to(z, hz, dst_bf, hdst, scale):
        for blk in range(SEQ // 512):
            cs = slice(blk * 512, (blk + 1) * 512)
            sq, hsq = C.sq512.next()
            P.op("pool", lambda e, sq=sq, cs=cs: e.tensor_tensor(out=sq[:], in0=z[:, cs], in1=z[:, cs], op=ALU.mult), reads=[hz], writes=[hsq])
            ps, hps = bkB.next()
            P.op("pe", lambda e, ps=ps, sq=sq: e.matmul(ps[:, :], lhsT=ones_b[:], rhs=sq[:], start=True, stop=True),
                 reads=[hones, hsq], writes=[hps])
            rs, hrs = C.rs512.next()
            P.op("act", lambda e, rs=rs, ps=ps: e.activation(out=rs[:], in_=ps[:, :], func=AF.Sqrt, bias=C.eps[:, 0:1], scale=1.0),
                 reads=[hps, C.heps], writes=[hrs])
            P.op("dve", lambda e, rs=rs: e.reciprocal(out=rs[:], in_=rs[:]), reads=[hrs], writes=[hrs])
            P.op("dve", lambda e, rs=rs, cs=cs: e.scalar_tensor_tensor(out=dst_bf[:, cs], in0=z[:, cs], scalar=scale, in1=rs[:],
                                                                      op0=ALU.mult, op1=ALU.mult), reads=[hz, hrs], writes=[hdst])

    C.sq512 = Pool2(P, 2, [128, 512], BF16, name="sq512")
    C.rs512 = Pool2(P, 2, [128, 512], F32, name="rs512")

    def pre(h):
        qT, hq = qTp.next(); kT, hk = kTp.next(); vT, hv = vTp.next(); qd, hqd = qdp.next()
        G, hG = Gp.next(); sg, hsg = sgp.next(); kbg, hkbg = kbgp.next(); kd, hkd = kdp.next(); vb, hvb = vbp.next()
        y, hy = tY
        conv_silu(h, 0, y, hy)
        l2norm_to(y, hy, qT, hq, QSCALE)
        conv_silu(h, 1, y, hy)
        l2norm_to(y, hy, kT, hk, 1.0)
        conv_silu(h, 2, y, hy)
        P.op("act", lambda e: e.copy(out=vT[:], in_=y[:]), reads=[hy], writes=[hv])
        P.dma("sp", G[:], gscr[h:h + 1, :].partition_broadcast(128), reads=[hgscr], writes=[hG])
        P.dma("sp", sg[:], ggT[h * 128:(h + 1) * 128, :], writes=[hsg])
        P.op("act", lambda e: e.activation(out=sg[:], in_=sg[:], func=AF.Silu), reads=[hsg], writes=[hsg])
        P.op("act", lambda e: e.activation(out=y[:], in_=G[:], func=AF.Exp), reads=[hG, hy], writes=[hy])
        P.op("pool", lambda e: e.tensor_tensor(out=qd[:], in0=qT[:], in1=y[:], op=ALU.mult), reads=[hq, hy], writes=[hqd])
        for g in range(0, NC_, 4):
            for (srcT, hsrc, outs) in ((kT, hk, ((kbg, hkbg, wcol, hwcol), (kd, hkd, kdcol, hkdcol))), (vT, hv, ((vb, hvb, bcol, hbcol),))):
                pt, hpt = C.tpsb.next()
                for j in range(4):
                    P.op("pe", lambda e, pt=pt, j=j, g=g, srcT=srcT: e.transpose(out=pt[:, j * 128:(j + 1) * 128],
                                                                                  in_=srcT[:, (g + j) * 128:(g + j + 1) * 128], identity=ident_b[:]),
                         reads=[hsrc, hidb], writes=[hpt])
                for (dst, hdst, col, hcol) in outs:
                    for j in range(4):
                        c = g + j
                        P.op("dve", lambda e, dst=dst, pt=pt, j=j, c=c, col=col: e.tensor_scalar(
                            out=dst[:, c, :], in0=pt[:, j * 128:(j + 1) * 128], scalar1=col[:, c * NH + h:c * NH + h + 1], scalar2=None,
                            op0=ALU.mult), reads=[hpt, hcol], writes=[hdst])
        return dict(h=h, qT=(qT, hq), kT=(kT, hk), qd=(qd, hqd), G=(G, hG), sg=(sg, hsg), kbg=(kbg, hkbg), kd=(kd, hkd), vb=(vb, hvb))

    def main(D):
        h = D["h"]
        (qT, hq), (kT, hk), (qd, hqd), (G, hG), (sg, hsg), (kbg, hkbg), (kd, hkd), (vb, hvb) = (
            D["qT"], D["kT"], D["qd"], D["G"], D["sg"], D["kbg"], D["kd"], D["vb"])
        S, hS = Sf.next(); Sbf, hSb = Sb.next()
        P.op("dve", lambda e: e.memset(S[:], 0.0), writes=[hS])
        P.op("pool", lambda e: e.memset(Sbf[:], 0.0), writes=[hSb])
        gc3 = gcol[:].rearrange("p (n h) -> p n h", h=NH)
        bc3 = bcol[:].rearrange("p (n h) -> p n h", h=NH)
        for c0 in range(0, NC_, 4):
            cs4 = slice(c0 * 128, (c0 + 4) * 128)
            G4 = G[:, cs4].rearrange("p (n c) -> p n c", c=128)
            d0, hd0 = d0p.next(); dec, hdec = decp.next(); decT, hdecT = decTp.next(); tmp, htmp = tmpp.next()
            gb = gc3[:, c0:c0 + 4, h:h + 1].broadcast_to([128, 4, 128])
            bb = bc3[:, c0:c0 + 4, h:h + 1].broadcast_to([128, 4, 128])
            P.op("dve", lambda e: e.tensor_tensor(out=d0[:], in0=G4, in1=gb, op=ALU.subtract), reads=[hG, hgcol], writes=[hd0])
            P.op("pool", lambda e: e.tensor_scalar(out=dec[:], in0=d0[:], scalar1=0.0, scalar2=None, op0=ALU.max), reads=[hd0], writes=[hdec])
            P.op("act", lambda e: e.activation(out=dec[:], in_=dec[:], func=AF.Exp, scale=-1.0), reads=[hdec], writes=[hdec])
            P.op("pool", lambda e: e.tensor_scalar(out=decT[:], in0=d0[:], scalar1=0.0, scalar2=None, op0=ALU.min), reads=[hd0], writes=[hdecT])
            P.op("act", lambda e: e.activation(out=decT[:], in_=decT[:], func=AF.Exp), reads=[hdecT], writes=[hdecT])
            kk, hkk = bkB.next()
            for i in range(4):
                cs = slice((c0 + i) * 128, (c0 + i + 1) * 128)
                P.op("pe", lambda e, i=i, cs=cs: e.matmul(kk[:, i * 128:(i + 1) * 128], lhsT=kT[:, cs], rhs=kT[:, cs], start=True, stop=True),
                     reads=[hk], writes=[hkk])
            YT, hYT = YTp.next()
            P.op("dve", lambda e: e.tensor_tensor(out=tmp[:], in0=kk[:, :].rearrange("p (n c) -> p n c", c=128), in1=dec[:], op=ALU.mult),
                 reads=[hkk, hdec], writes=[htmp])
            P.op("dve", lambda e: e.tensor_tensor(out=tmp[:], in0=tmp[:], in1=bb, op=ALU.mult), reads=[htmp, hbcol], writes=[htmp])
            P.op("dve", lambda e: e.tensor_tensor(out=YT[:], in0=tmp[:], in1=negsl[:].unsqueeze(1).broadcast_to([128, 4, 128]), op=ALU.mult),
                 reads=[htmp, hnegsl], writes=[hYT], tag=f"DBG YTwrite c{c0}" if (h == 0 and c0 == 0) else None)
            if C.dbg is not None:
                dbg = C.dbg
                tk = []
                for ii, (t_, h_) in enumerate(((d0, hd0), (dec, hdec), (decT, hdecT), (tmp, htmp), (YT, hYT))):
                    tk.append(P.dma("sp", dbg[:, ii, :], t_[:].rearrange("p n c -> p (n c)"), reads=[h_]))
                tk.append(P.dma("sp", dbg[:, 5, :], G[:, 0:512], reads=[hG]))
                tk.append(P.dma("sp", dbg[:, 6, 0:128], gcol[:], reads=[hgcol]))
                tk.append(P.dma("sp", dbg[:, 6, 128:256], bcol[:], reads=[hbcol]))
                tk.append(P.dma("sp", dbg[0:8, 7, :], gam[:, 0:512], reads=[hgam]))
                out_toks.extend(tk)
                return
            pa, hpa = bkA.next()
            for i in range(4):
                P.op("pe", lambda e, i=i, pa=pa, YT=YT: e.transpose(out=pa[:, i * 128:(i + 1) * 128], in_=YT[:, i, :], identity=ident_f[:]),
                     reads=[hYT, hidf], writes=[hpa], tag=f"DBG Xtr c{c0} i{i}" if (h == 0 and c0 == 0) else None)
            Y, hY = Yp.next()
            evac(Y[:], pa[:, :].rearrange("p (n c) -> p n c", c=128), [hpa], [hY], tag=f"Xevac h{h} c{c0}")
            if C.dbg2 is not None and c0 == C.dbg2_c0:
                out_toks.append(P.dma("sp", C.dbg2[:, 3, :], kk[:, :], reads=[hkk])) if False else None
                out_toks.append(P.dma("sp", C.dbg2[:, 4, :], dec[:].rearrange("p n c -> p (n c)"), reads=[hdec]))
                out_toks.append(P.dma("sp", C.dbg2[:, 5, :], d0[:].rearrange("p n c -> p (n c)"), reads=[hd0]))
                out_toks.append(P.dma("sp", C.dbg2[:, 6, :], kT[:, c0 * 128:c0 * 128 + 512], reads=[hk])) if False else None
                out_toks.append(P.dma("sp", C.dbg2[:, 0, :], Y[:].rearrange("p n c -> p (n c)"), reads=[hY]))
                out_toks.append(P.dma("sp", C.dbg2[:, 1, :], YT[:].rearrange("p n c -> p (n c)"), reads=[hYT]))
                out_toks.append(P.dma("sp", C.dbg2[:, 2, 0:128], ident_f[:], reads=[hidf]))
                return
            P4, hP4 = P4p.next()
            P.op("dve", lambda e: e.tensor_tensor(out=P4[:], in0=Y[:], in1=ident_f[:].unsqueeze(1).broadcast_to([128, 4, 128]), op=ALU.add),
                 reads=[hY, hidf], writes=[hP4])
            for k in range(1, 7):
                if k < 6:
                    pa, hpa = bkA.next()
                    for i in range(4):
                        P.op("pe", lambda e, i=i, pa=pa, Y=Y, YT=YT: e.matmul(pa[:, i * 128:(i + 1) * 128], lhsT=YT[:, i, :], rhs=Y[:, i, :],
                                                                              start=True, stop=True), reads=[hY, hYT], writes=[hpa])
                pb, hpb = bkA.next()
                for i in range(4):
                    P.op("pe", lambda e, i=i, pb=pb, Y=Y, YT=YT: e.matmul(pb[:, i * 128:(i + 1) * 128], lhsT=Y[:, i, :], rhs=YT[:, i, :],
                                                                          start=True, stop=True), reads=[hY, hYT], writes=[hpb])
                if k < 6:
                    Y2, hY2 = Yp.next()
                    evac(Y2[:], pa[:, :].rearrange("p (n c) -> p n c", c=128), [hpa], [hY2], tag=f"Y2evac h{h} c{c0} k{k}")
                YT2, hYT2 = YTp.next()
                evac(YT2[:], pb[:, :].rearrange("p (n c) -> p n c", c=128), [hpb], [hYT2], tag=f"YT2evac h{h} c{c0} k{k}")
                if k < 6:
                    Y, hY = Y2, hY2
                YT, hYT = YT2, hYT2
                pc, hpc = bkA.next()
                for i in range(4):
                    P.op("pe", lambda e, i=i, pc=pc, YT=YT, P4=P4: e.matmul(pc[:, i * 128:(i + 1) * 128], lhsT=YT[:, i, :], rhs=P4[:, i, :],
                                                                            start=True, stop=True), reads=[hYT, hP4], writes=[hpc])
                P.op("dve", lambda e, pc=pc, P4=P4: e.tensor_tensor(out=P4[:], in0=P4[:], in1=pc[:, :].rearrange("p (n c) -> p n c", c=128),
                                                                    op=ALU.add), reads=[hP4, hpc], writes=[hP4])
            TT, hTT = TTp.next()
            P.op("act", lambda e: e.copy(out=TT[:], in_=P4[:]), reads=[hP4], writes=[hTT])
            if C.stopg == "chain":
                out_toks.append(P.dma("sp", oT[0:128, 0:512], P4[:].rearrange("p n c -> p (n c)"), reads=[hP4]))
                return
            pq, hpq = bkB.next()
            for i in range(4):
                cs = slice((c0 + i) * 128, (c0 + i + 1) * 128)
                P.op("pe", lambda e, i=i, cs=cs, pq=pq: e.matmul(pq[:, i * 128:(i + 1) * 128], lhsT=kT[:, cs], rhs=qT[:, cs], start=True, stop=True),
                     reads=[hk, hq], writes=[hpq])
            qkT, hqkT = qkTp.next()
            P.op("dve", lambda e, pq=pq: e.tensor_tensor(out=tmp[:], in0=pq[:, :].rearrange("p (n c) -> p n c", c=128), in1=decT[:], op=ALU.mult),
                 reads=[hpq, hdecT], writes=[htmp])
            P.op("dve", lambda e, qkT=qkT: e.tensor_tensor(out=qkT[:], in0=tmp[:], in1=triu[:].unsqueeze(1).broadcast_to([128, 4, 128]), op=ALU.mult),
                 reads=[htmp, htriu], writes=[hqkT])
            pu, hpu = bkB.next()
            for i in range(4):
                P.op("pe", lambda e, i=i, pu=pu, TT=TT: e.matmul(pu[:, i * 128:(i + 1) * 128], lhsT=TT[:, i, :], rhs=vb[:, c0 + i, :], start=True, stop=True),
                     reads=[hTT, hvb], writes=[hpu])
            u0, hu0 = u0p.next()
            evac(u0[:], pu[:, :].rearrange("p (n c) -> p n c", c=128), [hpu], [hu0])
            pw, hpw = bkB.next()
            for i in range(4):
                P.op("pe", lambda e, i=i, pw=pw, TT=TT: e.matmul(pw[:, i * 128:(i + 1) * 128], lhsT=kbg[:, c0 + i, :], rhs=TT[:, i, :], start=True, stop=True),
                     reads=[hTT, hkbg], writes=[hpw])
            wT, hwT = wTp.next()
            evac(wT[:], pw[:, :].rearrange("p (n c) -> p n c", c=128), [hpw], [hwT])
            if C.stopg == "prep":
                out_toks.append(P.dma("sp", oT[0:128, 0:512], u0[:].rearrange("p n c -> p (n c)"), reads=[hu0]))
                return
            for i in range(4):
                c = c0 + i
                cs = slice(c * 128, (c + 1) * 128)
                p1, hp1 = bkS.next()
                P.op("pe", lambda e, i=i, p1=p1, wT=wT, Sbf=Sbf: e.matmul(p1[:, 0:128], lhsT=wT[:, i, :], rhs=Sbf[:], start=True, stop=True),
                     reads=[hwT, hSb], writes=[hp1])
                u, hu = up.next()
                P.op("dve", lambda e, i=i, u=u, u0=u0, p1=p1: e.tensor_tensor(out=u[:], in0=u0[:, i, :], in1=p1[:, 0:128], op=ALU.subtract),
                     reads=[hu0, hp1], writes=[hu])
                import os as _os
                SO = int(_os.environ.get("MK_SCANOPS", "9"))
                if SO <= 2:
                    out_toks.append(P.dma("sp", oT[0:128, 0:128], u[:], reads=[hu])) if False else None
                    return
                P.op("pe", lambda e, p1=p1, Sbf=Sbf, cs=cs: e.matmul(p1[:, 128:256], lhsT=Sbf[:], rhs=qd[:, cs], start=True, stop=False),
                     reads=[hSb, hqd], writes=[hp1])
                P.op("pe", lambda e, i=i, p1=p1, u=u, qkT=qkT: e.matmul(p1[:, 128:256], lhsT=u[:], rhs=qkT[:, i, :], start=False, stop=True),
                     reads=[hu, hqkT], writes=[hp1])
                if SO <= 4:
                    return
                P.op("pe", lambda e, p1=p1, u=u, c=c: e.matmul(p1[:, 256:384], lhsT=kd[:, c, :], rhs=u[:], start=True, stop=True),
                     reads=[hkd, hu], writes=[hp1])
                if SO <= 5:
                    return
                ot, hot = otp.next()
                P.op("dve", lambda e, ot=ot, p1=p1: e.tensor_copy(out=ot[:], in_=p1[:, 128:256]), reads=[hp1], writes=[hot])
                if SO <= 6:
                    return
                S2, hS2 = Sf.next(); Sbf2, hSb2 = Sb.next()
                P.op("dve", lambda e, S2=S2, S=S, p1=p1, c=c: e.scalar_tensor_tensor(
                    out=S2[:], in0=S[:], scalar=elast[:, h * NC_ + c:h * NC_ + c + 1], in1=p1[:, 256:384], op0=ALU.mult, op1=ALU.add),
                    reads=[hS, helast, hp1], writes=[hS2])
                if SO <= 7:
                    return
                P.op("act", lambda e, Sbf2=Sbf2, S2=S2: e.copy(out=Sbf2[:], in_=S2[:]), reads=[hS2], writes=[hSb2])
                S, hS, Sbf, hSb = S2, hS2, Sbf2, hSb2
                if SO <= 8:
                    return
                if C.stopg == "scan_nonorm":
                    if _os.environ.get("MK_NODMA") != "1":
                        out_toks.append(P.dma("sp", oT[row0 + h * 128:row0 + (h + 1) * 128, cs], ot[:], reads=[hot]))
                    if i == int(_os.environ.get("MK_NCH", "1")):
                        return
                    continue
                gated_norm_out(P, C, ot, hot, sg[:, cs], hsg, nw[:, 0:1], hnw, oT[row0 + h * 128:row0 + (h + 1) * 128, cs], out_toks)
                if C.stopg == "scan1":
                    return
            if C.stopg == "grp":
                return

    Dn = pre(0)
    for h in range(NH):
        Dc = Dn
        if h + 1 < NH and C.dbg is None and C.dbg2 is None and C.stopg is None:
            Dn = pre(h + 1)
        main(Dc)
        if C.dbg is not None or C.dbg2 is not None or C.stopg is not None:
            break
    P.barrier()
    P.release(m_sec)


def host_consts_nsa():
    n = np.arange(127)[:, None]
    m = np.arange(32)[None, :]
    overlap = ((16 * n < 64 * m + 64) & (16 * n + 32 > 64 * m)).astype(np.float32)
    t = np.arange(SEQ)[None, :]
    cmask = (16 * n + 31 <= t).astype(np.float32)
    kk = np.arange(SEQ)[None, :]
    E = ((kk // 64) == np.arange(32)[:, None]).astype(np.float32)
    tt = np.arange(SEQ)
    cur = (tt // 64)[:, None]
    valid = (m <= cur)
    forced = ((m == 0) | (m == cur) | (m == cur - 1)) & valid
    keep = (valid & ~forced).astype(np.float32)
    add = np.where(forced, 1e4, np.where(valid, 0.0, -1e30)).astype(np.float32)
    lay = lambda a: np.ascontiguousarray(a.reshape(16, 128, 32).transpose(1, 0, 2).reshape(128, 512))
    k = np.arange(128)[:, None]
    q = np.arange(128)[None, :]
    kgt = (k > q).astype(np.float32)
    gsel = np.zeros((24, 24, 128), np.float32)
    for c in range(24):
        gsel[c, c, :] = 1.0
    return dict(overlap=overlap, cmask=cmask, E=E, keep=lay(keep), add=lay(add), valid=lay(valid.astype(np.float32)),
                kgt=kgt, gsel=gsel.reshape(24, 24 * 128))


NSA_CONSTS = [("overlap", [127, 32], BF16), ("cmask", [127, SEQ], BF16), ("E", [32, SEQ], BF16), ("keep", [128, 512], F32),
              ("add", [128, 512], F32), ("valid", [128, 512], F32), ("kgt", [128, 128], BF16), ("gsel", [24, 24 * 128], F32)]


def nsa_section(P, C, nc, nqT, kcT, vcT, ksT, vs, kwT, vw, ngT, pek, pev, wk1, wk2, wv1, wv2, oT, row0, out_toks):
    m_sec0 = P.mark()
    load_consts(P, C, nc, NSA_CONSTS)
    ident_f, hidf = C.cd["identf"]
    ones_b, hones = C.cd["ones"]
    tri_b, htri = C.cd["tri"]
    kgt_b, hkgt = C.cd["kgt"]
    overlap, hov = C.cd["overlap"]
    cmask, hcm = C.cd["cmask"]
    Ecst, hE = C.cd["E"]
    keep, hkeep = C.cd["keep"]
    addc, hadd = C.cd["add"]
    validc, hvalid = C.cd["valid"]
    gsel, hgsel = C.cd["gsel"]
    m_sec = P.mark()
    bkS = Rot(C.bk[0:2]); bkN = C.bk[2]; bkD = C.bk[3]; bkM = C.bk[4]; bkG = C.bk[5]; bkX = C.bk[6]
    sg = P.sb([24, SEQ], F32, name="nsg"); hsg = H()
    P.dma("sp", sg[:], ngT, writes=[hsg])
    P.op("act", lambda e: e.activation(out=sg[:], in_=sg[:], func=AF.Sigmoid), reads=[hsg], writes=[hsg])
    w1 = [(P.sb([128, 32, 128], BF16, name=f"nw1{i}"), H()) for i in range(2)]
    w2 = [(P.sb([128, 128], BF16, name=f"nw2{i}"), H()) for i in range(2)]
    peT = [(P.sb([128, 32], F32, name=f"npe{i}"), H()) for i in range(2)]
    for i, (wa, wb, pe) in enumerate(((wk1, wk2, pek), (wv1, wv2, pev))):
        P.dma("pool", w1[i][0][:], wa.rearrange("(l d) o -> d l o", d=128), writes=[w1[i][1]])
        P.dma("pool", w2[i][0][:], wb, writes=[w2[i][1]])
        P.dma("sp", peT[i][0][:], pe, writes=[peT[i][1]])
    xf = P.sb([128, SEQ], F32, name="nxf"); hxf = H()
    Z = P.sb([128, 32, 127], BF16, name="nZ"); hZ = H()
    hid = P.sb([128, 127], BF16, name="nhid"); hhid = H()
    qT = P.sb([128, 4, SEQ], BF16, name="nq"); hq = H()
    ksb = P.sb([128, SEQ], BF16, name="nks"); hks = H()
    kwb = P.sb([128, SEQ], BF16, name="nkw"); hkw = H()
    vsb = P.sb([128, NTT, 128], BF16, name="nvs"); hvs = H()
    vwb = P.sb([128, NTT, 128], BF16, name="nvw"); hvw = H()
    kcm = P.sb([128, 127], BF16, name="nkcm"); hkcm = H()
    vcm = P.sb([127, 128], BF16, name="nvcm"); hvcm = H()
    PTp = Pool2(P, 3, [128, 4, 128], BF16, name="nPT")
    mkp = Pool2(P, 2, [128, 128], BF16, name="nmk")
    rdp = Pool2(P, 2, [128, 4, 128], F32, name="nrd")
    accp = Pool2(P, 2, [128, 4, 128], F32, name="nacc")
    impt = P.sb([32, 4, 128], F32, name="nimpt"); himpt = H()
    impT = P.sb([32, 128], F32, name="nimpT"); himpT = H()
    score = P.sb([128, 32], F32, name="nscore"); hscore = H()
    mx8 = P.sb([128, 8], F32, name="nmx8"); hmx8 = H()
    sel = P.sb([128, 32], F32, name="nsel"); hsel = H()
    selT = P.sb([32, 128], BF16, name="nselT"); hselT = H()

    def finalize(num, hnum, den, hden, br, gl, it, acc, hacc, first):
        rd, hrd = rdp.next()
        d3 = den[0][:, :].rearrange("p (h t) -> p h t", t=128)
        n3 = num[0][:, :].rearrange("p (h t) -> p h t", t=128)
        P.op("dve", lambda e: e.tensor_scalar(out=rd[:], in0=d3, scalar1=1e-30, scalar2=None, op0=ALU.max), reads=[hden], writes=[hrd])
        P.op("dve", lambda e: e.reciprocal(out=rd[:], in_=rd[:]), reads=[hrd], writes=[hrd])
        gp, hgp = bkG
        for p in range(4):
            r = br * 8 + gl * 4 + p
            P.op("pe", lambda e, p=p, r=r: e.matmul(gp[:, p * 128:(p + 1) * 128], lhsT=gsel[:, r * 128:(r + 1) * 128],
                                                    rhs=sg[:, it * 128:(it + 1) * 128], start=True, stop=True),
                 reads=[hgsel, hsg], writes=[hgp])
        rg, hrg = rdp.next()
        P.op("dve", lambda e: e.tensor_tensor(out=rg[:], in0=rd[:], in1=gp[:, :].rearrange("p (h t) -> p h t", t=128), op=ALU.mult),
             reads=[hrd, hgp], writes=[hrg])
        if first:
            P.op("dve", lambda e: e.tensor_tensor(out=acc[:], in0=n3, in1=rg[:], op=ALU.mult), reads=[hnum, hrg], writes=[hacc])
        else:
            P.op("dve", lambda e: e.tensor_tensor(out=rg[:], in0=n3, in1=rg[:], op=ALU.mult), reads=[hnum, hrg], writes=[hrg])
            P.op("pool", lambda e: e.tensor_tensor(out=acc[:], in0=acc[:], in1=rg[:], op=ALU.add), reads=[hacc, hrg], writes=[hacc])
        return rd, hrd

    for gl in range(2):
        rows = slice(gl * 128, (gl + 1) * 128)
        P.dma("pool", qT[:], nqT[gl * 512:(gl + 1) * 512, :].rearrange("(h d) t -> d h t", d=128), writes=[hq])
        P.dma("pool", ksb[:], ksT[rows, :], writes=[hks])
        P.dma("pool", kwb[:], kwT[rows, :], writes=[hkw])
        P.dma("pool", vsb[:], vs[:, rows].rearrange("(tt p) d -> p tt d", p=128), writes=[hvs])
        P.dma("pool", vwb[:], vw[:, rows].rearrange("(tt p) d -> p tt d", p=128), writes=[hvw])
        for which, src in ((0, kcT), (1, vcT)):
            P.dma("sp", xf[:], src[rows, :], writes=[hxf])
            for a in range(2):
                P.op("dve", lambda e, a=a, which=which: e.tensor_tensor(
                    out=Z[:, a * 16:(a + 1) * 16, :], in0=xf[:, 16 * a:16 * a + 2032].rearrange("p (n l) -> p l n", l=16),
                    in1=peT[which][0][:, a * 16:(a + 1) * 16].unsqueeze(2).broadcast_to([128, 16, 127]), op=ALU.add),
                    reads=[hxf, peT[which][1]], writes=[hZ])
            ph, hph = bkX
            for l in range(32):
                P.op("pe", lambda e, l=l, which=which: e.matmul(ph[:, 0:127], lhsT=w1[which][0][:, l, :], rhs=Z[:, l, :],
                                                              start=(l == 0), stop=(l == 31)), reads=[w1[which][1], hZ], writes=[hph])
            P.op("act", lambda e: e.activation(out=hid[:], in_=ph[:, 0:127], func=AF.Silu), reads=[hph], writes=[hhid])
            if which == 0:
                P.op("pe", lambda e: e.matmul(ph[:, 128:255], lhsT=w2[0][0][:], rhs=hid[:], start=True, stop=True),
                     reads=[w2[0][1], hhid], writes=[hph])
                P.op("dve", lambda e: e.tensor_copy(out=kcm[:], in_=ph[:, 128:255]), reads=[hph], writes=[hkcm])
            else:
                P.op("pe", lambda e: e.matmul(ph[0:127, 256:384], lhsT=hid[:], rhs=w2[1][0][:], start=True, stop=True),
                     reads=[w2[1][1], hhid], writes=[hph])
                P.op("dve", lambda e: e.tensor_copy(out=vcm[:], in_=ph[0:127, 256:384]), reads=[hph], writes=[hvcm])
        for it in range(NTT):
            tc_ = slice(it * 128, (it + 1) * 128)
            q4 = qT[:, :, tc_]
            acc, hacc = accp.next()
            S, hS = bkS.next()
            P.op("pe", lambda e, S=S: e.matmul(S[0:127, :], lhsT=kcm[:], rhs=q4, start=True, stop=True), reads=[hkcm, hq], writes=[hS])
            PT, hPT = PTp.next()
            P.op("act", lambda e, S=S, PT=PT: e.activation(out=PT[0:127], in_=S[0:127, :].rearrange("p (h t) -> p h t", t=128), func=AF.Exp, scale=QSCALE),
                 reads=[hS], writes=[hPT])
            P.op("dve", lambda e, PT=PT: e.tensor_tensor(out=PT[0:127], in0=PT[0:127],
                                                         in1=cmask[:, tc_].unsqueeze(1).broadcast_to([127, 4, 128]), op=ALU.mult),
                 reads=[hPT, hcm], writes=[hPT])
            P.op("pe", lambda e, PT=PT: e.matmul(bkN[0][:, :], lhsT=vcm[:], rhs=PT[0:127], start=True, stop=True), reads=[hvcm, hPT], writes=[bkN[1]])
            P.op("pe", lambda e, PT=PT: e.matmul(bkD[0][:, :], lhsT=ones_b[0:127, :], rhs=PT[0:127], start=True, stop=True),
                 reads=[hones, hPT], writes=[bkD[1]])
            P.op("pe", lambda e, PT=PT: e.matmul(bkX[0][0:32, :], lhsT=overlap[:], rhs=PT[0:127], start=True, stop=True),
                 reads=[hov, hPT], writes=[bkX[1]])
            rd, hrd = finalize(bkN, bkN[1], bkD, bkD[1], 0, gl, it, acc, hacc, True)
            P.op("dve", lambda e, rd=rd: e.tensor_tensor(out=impt[:], in0=bkX[0][0:32, :].rearrange("p (h t) -> p h t", t=128),
                                                         in1=rd[0:32], op=ALU.mult), reads=[bkX[1], hrd], writes=[himpt])
            P.op("dve", lambda e: e.tensor_reduce(out=impT[:], in_=impt[:].rearrange("p h t -> p t h"), axis=AX.X, op=ALU.add),
                 reads=[himpt], writes=[himpT])
            px, hpx = bkX
            P.op("pe", lambda e: e.transpose(out=px[:, 0:32], in_=impT[:], identity=ident_f[0:32, 0:32]), reads=[himpT, hidf], writes=[hpx])
            tb = slice(it * 32, (it + 1) * 32)
            P.op("dve", lambda e: e.tensor_tensor(out=score[:], in0=px[:, 0:32], in1=keep[:, tb], op=ALU.mult), reads=[hpx, hkeep], writes=[hscore])
            P.op("dve", lambda e: e.tensor_tensor(out=score[:], in0=score[:], in1=addc[:, tb], op=ALU.add), reads=[hscore, hadd], writes=[hscore])
            P.op("dve", lambda e: e.max(out=mx8[:], in_=score[:]), reads=[hscore], writes=[hmx8])
            P.op("dve", lambda e: e.tensor_scalar(out=sel[:], in0=score[:], scalar1=mx8[:, 7:8], scalar2=None, op0=ALU.is_ge),
                 reads=[hscore, hmx8], writes=[hsel])
            P.op("dve", lambda e: e.tensor_tensor(out=sel[:], in0=sel[:], in1=validc[:, tb], op=ALU.mult), reads=[hsel, hvalid], writes=[hsel])
            P.op("pe", lambda e: e.transpose(out=px[0:32, 128:256], in_=sel[:], identity=ident_f[:]), reads=[hsel, hidf], writes=[hpx])
            P.op("dve", lambda e: e.tensor_copy(out=selT[:], in_=px[0:32, 128:256]), reads=[hpx], writes=[hselT])
            for br, kb, vb_, hk_, hv_, jlist in ((1, ksb, vsb, hks, hvs, list(range(it + 1))),
                                                 (2, kwb, vwb, hkw, hvw, list(range(max(0, it - 4), it + 1)))):
                for ji, j in enumerate(jlist):
                    kc_ = slice(j * 128, (j + 1) * 128)
                    mk = None
                    if br == 1:
                        P.op("pe", lambda e, kc_=kc_: e.matmul(bkM[0][:, 0:128], lhsT=Ecst[:, kc_], rhs=selT[:], start=True, stop=True),
                             reads=[hE, hselT], writes=[bkM[1]])
                        mk, hmk = mkp.next()
                        if j == it:
                            P.op("dve", lambda e, mk=mk: e.tensor_tensor(out=mk[:], in0=bkM[0][:, 0:128], in1=tri_b[:], op=ALU.mult),
                                 reads=[bkM[1], htri], writes=[hmk])
                        else:
                            P.op("dve", lambda e, mk=mk: e.tensor_copy(out=mk[:], in_=bkM[0][:, 0:128]), reads=[bkM[1]], writes=[hmk])
                    else:
                        if j == it:
                            mk, hmk = tri_b, htri
                        elif j == it - 4:
                            mk, hmk = kgt_b, hkgt
                    S, hS = bkS.next()
                    P.op("pe", lambda e, S=S, kb=kb, kc_=kc_: e.matmul(S[:, :], lhsT=kb[:, kc_], rhs=q4, start=True, stop=True),
                         reads=[hk_, hq], writes=[hS])
                    PT, hPT = PTp.next()
                    P.op("act", lambda e, S=S, PT=PT: e.activation(out=PT[:], in_=S[:, :].rearrange("p (h t) -> p h t", t=128), func=AF.Exp, scale=QSCALE),
                         reads=[hS], writes=[hPT])
                    if mk is not None:
                        P.op("pool" if br == 2 else "dve", lambda e, PT=PT, mk=mk: e.tensor_tensor(
                            out=PT[:], in0=PT[:], in1=mk[:].unsqueeze(1).broadcast_to([128, 4, 128]), op=ALU.mult),
                            reads=[hPT, hmk], writes=[hPT])
                    P.op("pe", lambda e, PT=PT, vb_=vb_, j=j, ji=ji, jlist=jlist: e.matmul(
                        bkN[0][:, :], lhsT=vb_[:, j, :], rhs=PT[:], start=(ji == 0), stop=(ji == len(jlist) - 1)),
                        reads=[hv_, hPT], writes=[bkN[1]])
                    P.op("pe", lambda e, PT=PT, ji=ji, jlist=jlist: e.matmul(
                        bkD[0][:, :], lhsT=ones_b[:], rhs=PT[:], start=(ji == 0), stop=(ji == len(jlist) - 1)),
                        reads=[hones, hPT], writes=[bkD[1]])
                finalize(bkN, bkN[1], bkD, bkD[1], br, gl, it, acc, hacc, False)
            r0 = row0 + gl * 512
            out_toks.append(P.dma("sp", oT[r0:r0 + 512, tc_].rearrange("(h d) t -> d h t", d=128), acc[:], reads=[hacc]))
    P.barrier()
    P.release(m_sec0)


M0_CONSTS = [("ident", [128, 128], BF16), ("identf", [128, 128], F32), ("ones", [128, 128], BF16), ("tri", [128, 128], BF16),
             ("trif", [128, 128], F32), ("negsl", [128, 128], F32), ("reset128", [8, SEQ], F32), ("one8", [8, 1], F32)]


def build_M0(do_gdn=True, do_nsa=True):
    nc, P = new_prog()
    C = Ctx()
    gqkvT = din(nc, "gqkvT", [3072, SEQ]); gconvw = din(nc, "gconvw", [128, 96])
    gaT = din(nc, "gaT", [8, SEQ]); gbT = din(nc, "gbT", [8, SEQ]); galog = din(nc, "galog", [8, 1]); gdtb = din(nc, "gdtb", [8, 1])
    ggT = din(nc, "ggT", [1024, SEQ]); gnorm = din(nc, "gnorm", [128, 1])
    nqT = din(nc, "nqT", [1024, SEQ]); kcT = din(nc, "kcT", [256, SEQ]); vcT = din(nc, "vcT", [256, SEQ])
    ksT = din(nc, "ksT", [256, SEQ]); vs = din(nc, "vs", [SEQ, 256]); kwT = din(nc, "kwT", [256, SEQ]); vw = din(nc, "vw", [SEQ, 256])
    ngT = din(nc, "ngT", [24, SEQ]); pek = din(nc, "pek", [128, 32]); pev = din(nc, "pev", [128, 32])
    wk1 = din(nc, "wk1", [4096, 128]); wk2 = din(nc, "wk2", [128, 128]); wv1 = din(nc, "wv1", [4096, 128]); wv2 = din(nc, "wv2", [128, 128])
    oT = dout(nc, "oT", [2048, SEQ])
    import os as _os
    C.dbg = dout(nc, "dbg", [128, 8, 512]) if _os.environ.get("MK_DBG") else None
    C.dbg2 = dout(nc, "dbg2", [128, 8, 512]) if _os.environ.get("MK_DBG2") else None
    C.dbg2_c0 = int(_os.environ.get("MK_DBG2", "0") or 0)
    C.stopg = _os.environ.get("MK_STOPG") or None
    load_consts(P, C, nc, M0_CONSTS)
    mixer_pools0(P, C)
    out_toks = []
    if do_gdn:
        gdn_section(P, C, nc, gqkvT, gconvw, gaT, gbT, galog, gdtb, ggT, gnorm, oT, 0, out_toks)
    if do_nsa:
        nsa_section(P, C, nc, nqT, kcT, vcT, ksT, vs, kwT, vw, ngT, pek, pev, wk1, wk2, wv1, wv2, oT, 1024, out_toks)
    P.wait_all("sp", out_toks)
    P.build()
    return nc


def _T(a):
    return np.ascontiguousarray(a.T)


def _m0_inputs(proj_b, d, hh):
    hsel = np.arange(hh * 8, hh * 8 + 8)
    qkv = proj_b[:, 0:6144].reshape(SEQ, 3, 16, 128)[:, :, hsel]
    gqkvT = np.ascontiguousarray(qkv.transpose(1, 2, 3, 0).reshape(3072, SEQ))
    cw = d["gdn_conv_w"][0].reshape(4, 3, 16, 128)[:, :, hsel]
    gconvw = np.ascontiguousarray(cw.transpose(3, 1, 2, 0).reshape(128, 96))
    ga = proj_b[:, 6144:6160][:, hsel]
    gb = proj_b[:, 6160:6176][:, hsel]
    gg = proj_b[:, 6176:8224][:, hh * 1024:(hh + 1) * 1024]
    ins = dict(gqkvT=gqkvT, gconvw=gconvw, gaT=_T(ga), gbT=_T(gb),
               galog=np.ascontiguousarray(d["gdn_a_log"][0, hsel].reshape(8, 1)),
               gdtb=np.ascontiguousarray(d["gdn_dt_bias"][0, hsel].reshape(8, 1)),
               ggT=_T(gg), gnorm=np.ascontiguousarray(d["gdn_norm"][0].reshape(128, 1)))
    gsel_ = np.arange(hh * 2, hh * 2 + 2)
    nq = proj_b[:, 8224:10272].reshape(SEQ, 4, 4, 128)[:, gsel_]
    ins["nqT"] = np.ascontiguousarray(nq.transpose(1, 2, 3, 0).reshape(1024, SEQ))
    nkv = proj_b[:, 10272:13344].reshape(SEQ, 6, 4, 128)[:, :, gsel_]
    fm = lambda s_: np.ascontiguousarray(nkv[:, s_].transpose(1, 2, 0).reshape(256, SEQ))
    tm = lambda s_: np.ascontiguousarray(nkv[:, s_].reshape(SEQ, 256))
    ins.update(kcT=fm(0), vcT=fm(1), ksT=fm(2), vs=tm(3), kwT=fm(4), vw=tm(5))
    ng = proj_b[:, 13344:13392].reshape(SEQ, 3, 4, 4)[:, :, gsel_]
    ins["ngT"] = np.ascontiguousarray(ng.reshape(SEQ, 24).T)
    ins.update(pek=_T(d["nsa_pe_k"][0]), pev=_T(d["nsa_pe_v"][0]), wk1=d["nsa_cmp_k1"][0], wk2=d["nsa_cmp_k2"][0],
               wv1=d["nsa_cmp_v1"][0], wv2=d["nsa_cmp_v2"][0])
    hc = host_consts0()
    hc.update(host_consts_nsa())
    for nm, shape, dt in M0_CONSTS + NSA_CONSTS:
        key = {"identf": "ident", "trif": "tri"}.get(nm, nm)
        ins["c_" + nm] = hc[key]
    return ins


def _m1_inputs(proj_b, d, hh):
    hs = slice(hh * 1024, (hh + 1) * 1024)
    hq, hf, hi_, hg = proj_b[:, 0:2048], proj_b[:, 2048:4096], proj_b[:, 4096:6144], proj_b[:, 6144:8192]
    fq, fk, fvv = proj_b[:, 8192:10240], proj_b[:, 10240:12288], proj_b[:, 12288:14336]
    ffg = proj_b[:, 14336:14352]
    lg = d["hgrn_lb_logits"].reshape(2, 16, 128)[:, hh * 8:(hh + 1) * 8]
    lbl = np.ascontiguousarray(lg.transpose(2, 0, 1).reshape(128, 16))
    ins = dict(fqT=_T(fq[:, hs]), fkT=_T(fk[:, hs]), fv=np.ascontiguousarray(fvv[:, hs]), ffT=_T(ffg[:, hh * 8:(hh + 1) * 8]),
               fbias=np.ascontiguousarray(d["fox_f_bias"][0, hh * 8:(hh + 1) * 8].reshape(8, 1)),
               hqT=_T(hq[:, hs]), hfT=_T(hf[:, hs]), hgT=_T(hg[:, hs]), hi=np.ascontiguousarray(hi_[:, hs]),
               lbl=lbl, hnorm=np.ascontiguousarray(d["hgrn_norm"][0].reshape(128, 1)))
    hc = host_consts()
    for nm, shape, dt in M1_CONSTS:
        key = {"identf": "ident"}.get(nm, nm)
        ins["c_" + nm] = hc[key]
    return ins


def _c_inputs(li, d, o_full, x_full, b, t0):
    def halo(a):
        if t0 == 0:
            return np.concatenate([np.zeros((128, a.shape[-1]), np.float32), a[b, 0:1024]], 0)
        return a[b, t0 - 128:t0 + 1024]
    return _T(halo(o_full)), np.ascontiguousarray(halo(x_full)), _T(d["p"][li, b, t0:t0 + 1024])


def _c_weights(li, d):
    if li == 0:
        W_out, gpost = d["ab_w_out"][0], d["ab_norm_post"][0]
    else:
        W_out, gpost = d["cd_w_out"][0], d["cd_norm_post"][0]
    gains = np.stack([gpost, d["ffn_norm_pre"][li], d["ffn_norm_post"][li], d["ple_gate_norm"][li], d["ple_norm_post"][li]])
    cw = d["ffn_conv_w"][li]
    convw = np.ascontiguousarray(cw.reshape(3, 172, 128).transpose(2, 1, 0).reshape(128, 172 * 3))
    convb = np.ascontiguousarray(d["ffn_conv_b"][li].reshape(172, 128).T)
    return {"W_out": W_out, "W_up": d["ffn_w_up"][li], "W_down": d["ffn_w_down"][li], "Wg": d["ple_w_gate"][li],
            "Wp": d["ple_w_proj"][li], "gains": np.ascontiguousarray(gains), "convw": convw, "convb": convb,
            "ident": np.eye(128, dtype=np.float32)}


def _run_A(ncA, slabs, N, x_full, gain, W):
    xf = x_full.reshape(-1, D_MODEL)
    ident = np.eye(128, dtype=np.float32)
    ins = [{"x": xf[c * TOK:(c + 1) * TOK], "gain": np.ascontiguousarray(gain.reshape(1, D_MODEL)), "W": W, "ident": ident}
           for c in range(8)]
    res = run_bass_kernel_spmd(ncA, ins, core_ids=list(range(8))).results
    nf, nt, posF, posT = slab_layout(slabs)
    proj = np.empty((8 * TOK, N), np.float32)
    for c in range(8):
        oF, oT = res[c]["outF"], res[c]["outT"]
        blk = proj[c * TOK:(c + 1) * TOK]
        for (c0, w, m, tag) in slabs:
            if m == "F":
                blk[:, c0:c0 + w] = oF[posF[c0]:posF[c0] + w].T
            else:
                blk[:, c0:c0 + w] = oT[:, posT[c0]:posT[c0] + w]
    return proj.reshape(4, SEQ, N)


def _run_M(ncM, make_inputs, proj, d):
    ins = [make_inputs(proj[c // 2], d, c % 2) for c in range(8)]
    res = run_bass_kernel_spmd(ncM, ins, core_ids=list(range(8))).results
    o_full = np.empty((4, SEQ, D_MODEL), np.float32)
    for c in range(8):
        b, hh = c // 2, c % 2
        oT = res[c]["oT"]
        o_full[b][:, hh * 1024:(hh + 1) * 1024] = oT[0:1024].T
        o_full[b][:, 2048 + hh * 1024:2048 + (hh + 1) * 1024] = oT[1024:2048].T
    return o_full


def _run_C(ncC, li, d, o_full, x_full):
    w = _c_weights(li, d)
    ins = []
    for b in range(4):
        parts = [_c_inputs(li, d, o_full, x_full, b, t0) for t0 in (0, 1024)]
        ins.append(dict(oT=np.concatenate([p[0] for p in parts], 0), xin=np.concatenate([p[1] for p in parts], 0),
                        pT=np.concatenate([p[2] for p in parts], 0), **w))
    res = run_bass_kernel_spmd(ncC, ins, core_ids=list(range(4))).results
    return np.stack([res[b]["xout"] for b in range(4)], 0)


def kernel(**inputs):
    d = {k: np.asarray(v) for k, v in inputs.items()}
    x = np.ascontiguousarray(d["x"], dtype=np.float32)
    ncC = build_C(2)
    ncA0 = build_A(AB_SLABS, 13392)
    proj = _run_A(ncA0, AB_SLABS, 13392, x, d["ab_norm_pre"][0], d["ab_w_in"][0])
    o_full = _run_M(build_M0(), _m0_inputs, proj, d)
    del proj
    x = _run_C(ncC, 0, d, o_full, x)
    ncA1 = build_A(CD_SLABS, 14352)
    proj = _run_A(ncA1, CD_SLABS, 14352, x, d["cd_norm_pre"][0], d["cd_w_in"][0])
    o_full = _run_M(build_M1(), _m1_inputs, proj, d)
    del proj
    x = _run_C(ncC, 1, d, o_full, x)
    return x.astype(np.float32)
```
